# Optimizing a Trainium2 kernel written in Bass

```python
import math
import jax, jax.numpy as jnp
from jax import lax
import numpy as np


D_MODEL = 1024
BATCH = 4
SEQ = 4096
DEPTH = 4

N_MIXERS = 3
HEAD_DIM = 64
DIL_PATTERNS = ((128, 1), (512, 4), (2048, 16))
N_DIL_GROUPS = len(DIL_PATTERNS)
DIL_HEADS = D_MODEL // HEAD_DIM
DIFF_HEADS = D_MODEL // (2 * HEAD_DIM)
DIFF_V_DIM = 2 * HEAD_DIM
N_BIAS_HEADS = DIL_HEADS
N_BUCKETS = 32
MAX_DISTANCE = 2048
Q_BLOCK = 128
SSM_D_INNER = 2 * D_MODEL
SSM_HEAD_DIM = 64
SSM_HEADS = SSM_D_INNER // SSM_HEAD_DIM
SSM_GROUPS = 8
SSM_D_STATE = 128
SSM_CONV = 4
SSM_CHUNK = 128
SSM_CONV_DIM = SSM_D_INNER + 2 * SSM_GROUPS * SSM_D_STATE
SSM_IN_DIM = SSM_D_INNER + SSM_CONV_DIM + SSM_HEADS
D_FF = 4 * D_MODEL
NORM_EPS = 1e-5
DT_MIN = 1e-3
DT_MAX = 1e-1

kernel_name = 'hybrid_dilated_diff_ssd_trunk'


def rms_norm(x, gain):
    xf = x.astype(jnp.float32)
    y = xf * lax.rsqrt(jnp.mean(xf * xf, axis=-1, keepdims=True) + NORM_EPS)
    return (y * gain.astype(jnp.float32)).astype(x.dtype)


def rel_bucket(dist):
    exact = N_BUCKETS // 2
    d = jnp.maximum(dist, 1).astype(jnp.float32)
    large = exact + (jnp.log(d / exact) / math.log(MAX_DISTANCE / exact) * (N_BUCKETS - exact)).astype(jnp.int32)
    large = jnp.minimum(large, N_BUCKETS - 1)
    return jnp.where(dist < exact, dist, large)


def dilated_group(q, k, v, rel_bias, window, dilation):
    b, s, h, d = q.shape
    n = window // dilation
    length = s // dilation
    nb = -(-length // n)
    lp = nb * n

    def to_blocks(t):
        t = jnp.moveaxis(t.reshape(b, length, dilation, h, d), 2, 1)
        t = jnp.pad(t, ((0, 0), (0, 0), (0, lp - length), (0, 0), (0, 0)))
        return t.reshape(b, dilation, nb, n, h, d)

    def with_prev(t):
        prev = jnp.pad(t, ((0, 0), (0, 0), (1, 0), (0, 0), (0, 0), (0, 0)))[:, :, :-1]
        return jnp.concatenate([prev, t], axis=3)

    qb = to_blocks(q).astype(jnp.float32)
    kw = with_prev(to_blocks(k)).astype(jnp.float32)
    vw = with_prev(to_blocks(v)).astype(jnp.float32)

    qi = jnp.arange(n)[:, None]
    kj = jnp.arange(2 * n)[None, :]
    steps = n + qi - kj
    band = (steps >= 0) & (steps <= n)
    blk = jnp.arange(nb)[:, None, None]
    valid = band[None] & ((blk - 1) * n + kj[None] >= 0)
    bias = rel_bias[rel_bucket(jnp.clip(steps, 0, n) * dilation)].astype(jnp.float32)
    bias = jnp.moveaxis(bias, -1, 0)

    scores = jnp.einsum('brnqhd,brnkhd->brnhqk', qb, kw) * (HEAD_DIM ** -0.5) + bias[None, None, None]
    scores = jnp.where(valid[None, None, :, None], scores, -jnp.inf)
    m = jnp.max(scores, axis=-1, keepdims=True)
    p = jnp.exp(scores - m)
    den = jnp.sum(p, axis=-1, keepdims=True)
    out = jnp.einsum('brnhqk,brnkhd->brnqhd', p / den, vw)
    lse = (m + jnp.log(den))[..., 0]

    out = out.reshape(b, dilation, lp, h, d)[:, :, :length]
    out = jnp.moveaxis(out, 1, 2).reshape(b, s, h, d)
    lse = jnp.moveaxis(lse, 3, 4).reshape(b, dilation, lp, h)[:, :, :length]
    lse = jnp.moveaxis(lse, 1, 2).reshape(b, s, h)
    return out, lse


def dilated_mixer(h_in, w_qkv, w_o, rel_bias):
    b, s, _ = h_in.shape
    qkv = (h_in @ w_qkv).reshape(b, s, N_DIL_GROUPS, 3, DIL_HEADS, HEAD_DIM)
    outs, lses = [], []
    for g, (window, dilation) in enumerate(DIL_PATTERNS):
        o, l = dilated_group(qkv[:, :, g, 0], qkv[:, :, g, 1], qkv[:, :, g, 2], rel_bias, window, dilation)
        outs.append(o)
        lses.append(l)
    weights = jax.nn.softmax(jnp.stack(lses), axis=0)
    o = jnp.einsum('gbsh,gbshd->bshd', weights, jnp.stack(outs))
    return o.reshape(b, s, DIL_HEADS * HEAD_DIM).astype(h_in.dtype) @ w_o


def diff_lambda_init(layer):
    return 0.8 - 0.6 * math.exp(-0.3 * layer)


def diff_mixer(h_in, w_qkv, lam_q1, lam_k1, lam_q2, lam_k2, subln, w_o, rel_bias, lambda_init):
    b, s, _ = h_in.shape
    qkv = h_in @ w_qkv
    width = DIFF_HEADS * 2 * HEAD_DIM
    q = qkv[..., :width].reshape(b, s, DIFF_HEADS, 2, HEAD_DIM).astype(jnp.float32)
    k = qkv[..., width:2 * width].reshape(b, s, DIFF_HEADS, 2, HEAD_DIM).astype(jnp.float32)
    v = qkv[..., 2 * width:].reshape(b, s, DIFF_HEADS, DIFF_V_DIM).astype(jnp.float32)
    lam = (jnp.exp(jnp.sum(lam_q1.astype(jnp.float32) * lam_k1.astype(jnp.float32)))
           - jnp.exp(jnp.sum(lam_q2.astype(jnp.float32) * lam_k2.astype(jnp.float32))) + lambda_init)
    bias_cols = rel_bias.reshape(N_BUCKETS, 2, DIFF_HEADS).astype(jnp.float32)
    nblk = s // Q_BLOCK
    qb = jnp.moveaxis(q.reshape(b, nblk, Q_BLOCK, DIFF_HEADS, 2, HEAD_DIM), 1, 0)
    starts = jnp.arange(nblk) * Q_BLOCK
    key_pos = jnp.arange(s)

    def block(args):
        q_blk, start = args
        dist = (start + jnp.arange(Q_BLOCK))[:, None] - key_pos[None, :]
        bias = bias_cols[rel_bucket(jnp.maximum(dist, 0))]
        scores = jnp.einsum('bqhmd,bkhmd->bhmqk', q_blk, k) * (HEAD_DIM ** -0.5) + bias.transpose(3, 2, 0, 1)
        scores = jnp.where(dist >= 0, scores, -jnp.inf)
        p = jax.nn.softmax(scores, axis=-1)
        a = p[:, :, 0] - lam * p[:, :, 1]
        return jnp.einsum('bhqk,bkhe->bqhe', a, v)

    o = lax.map(block, (qb, starts))
    o = jnp.moveaxis(o, 0, 1).reshape(b, s, DIFF_HEADS, DIFF_V_DIM)
    o = rms_norm(o, subln) * (1.0 - lambda_init)
    return o.reshape(b, s, DIFF_HEADS * DIFF_V_DIM).astype(h_in.dtype) @ w_o


def ssd_chunked(x, a, bmat, cmat):
    b, s, h, p = x.shape
    g, n = bmat.shape[2], bmat.shape[3]
    hg = h // g
    c, l = s // SSM_CHUNK, SSM_CHUNK
    x = x.reshape(b, c, l, g, hg, p)
    a = jnp.moveaxis(a.reshape(b, c, l, g, hg), 2, -1)
    bmat = bmat.reshape(b, c, l, g, n)
    cmat = cmat.reshape(b, c, l, g, n)
    a_cs = jnp.cumsum(a, axis=-1)
    causal = jnp.tril(jnp.ones((l, l), dtype=bool))
    seg = a_cs[..., :, None] - a_cs[..., None, :]
    decay = jnp.where(causal, jnp.exp(jnp.where(causal, seg, 0.0)), 0.0)
    cb = jnp.einsum('bclgn,bcsgn->bcgls', cmat, bmat)
    y_diag = jnp.einsum('bcgjls,bcsgjp->bclgjp', cb[:, :, :, None] * decay, x)
    state_decay = jnp.exp(a_cs[..., -1:] - a_cs)
    chunk_states = jnp.einsum('bclgn,bcgjl,bclgjp->bcgjpn', bmat, state_decay, x)
    chunk_decay = jnp.exp(a_cs[..., -1])

    def step(state, inp):
        st, dec = inp
        return state * dec[..., None, None] + st, state

    init = jnp.zeros((b, g, hg, p, n), jnp.float32)
    _, prev = lax.scan(step, init, (jnp.moveaxis(chunk_states, 1, 0), jnp.moveaxis(chunk_decay, 1, 0)))
    prev = jnp.moveaxis(prev, 0, 1)
    y_off = jnp.einsum('bclgn,bcgjpn,bcgjl->bclgjp', cmat, prev, jnp.exp(a_cs))
    return (y_diag + y_off).reshape(b, s, h, p)


def ssd_mixer(h_in, w_in, conv_w, conv_b, dt_bias, a_log, d_skip, gate_norm, w_out):
    b, s, _ = h_in.shape
    proj = h_in @ w_in
    z = proj[..., :SSM_D_INNER]
    xbc = proj[..., SSM_D_INNER:SSM_D_INNER + SSM_CONV_DIM]
    dt = proj[..., SSM_D_INNER + SSM_CONV_DIM:]
    xbc = lax.conv_general_dilated(xbc, conv_w[:, None, :], window_strides=(1,), padding=((SSM_CONV - 1, 0),),
                                   dimension_numbers=('NWC', 'WIO', 'NWC'), feature_group_count=SSM_CONV_DIM)
    xbc = jax.nn.silu((xbc + conv_b).astype(jnp.float32))
    xs = xbc[..., :SSM_D_INNER].reshape(b, s, SSM_HEADS, SSM_HEAD_DIM)
    bmat = xbc[..., SSM_D_INNER:SSM_D_INNER + SSM_GROUPS * SSM_D_STATE].reshape(b, s, SSM_GROUPS, SSM_D_STATE)
    cmat = xbc[..., SSM_D_INNER + SSM_GROUPS * SSM_D_STATE:].reshape(b, s, SSM_GROUPS, SSM_D_STATE)
    dt = jax.nn.softplus(dt.astype(jnp.float32) + dt_bias.astype(jnp.float32))
    a = -jnp.exp(a_log.astype(jnp.float32))
    y = ssd_chunked(xs * dt[..., None], dt * a, bmat, cmat) + xs * d_skip.astype(jnp.float32)[:, None]
    y = y.reshape(b, s, SSM_D_INNER) * jax.nn.silu(z.astype(jnp.float32))
    y = rms_norm(y.reshape(b, s, SSM_GROUPS, -1), gate_norm.reshape(SSM_GROUPS, -1)).reshape(b, s, SSM_D_INNER)
    return y.astype(h_in.dtype) @ w_out


def sq_relu_mlp(h, w_up, w_down):
    u = jnp.maximum(h @ w_up, 0)
    return (u * u) @ w_down


def _dense(k, fan_in, fan_out):
    return jax.random.normal(k, (fan_in, fan_out), jnp.float32) * fan_in ** -0.5


def _gain(k, n):
    return 1.0 + 0.02 * jax.random.normal(k, (n,), jnp.float32)


def setup_inputs(seed: int = 0) -> dict:
    key = jax.random.key(seed)
    ks = iter(jax.random.split(key, 64))
    p = {}
    p['x'] = jax.random.normal(next(ks), (BATCH, SEQ, D_MODEL), jnp.float32)
    p['rel_bias'] = 0.2 * jax.random.normal(next(ks), (N_BUCKETS, N_BIAS_HEADS), jnp.float32)
    for i in range(DEPTH):
        pre = f'l{i}_'
        kind = i % N_MIXERS
        p[pre + 'mix_norm'] = _gain(next(ks), D_MODEL)
        if kind == 0:
            p[pre + 'dil_w_qkv'] = _dense(next(ks), D_MODEL, N_DIL_GROUPS * 3 * DIL_HEADS * HEAD_DIM)
            p[pre + 'dil_w_o'] = _dense(next(ks), DIL_HEADS * HEAD_DIM, D_MODEL)
        elif kind == 1:
            p[pre + 'diff_w_qkv'] = _dense(next(ks), D_MODEL, 3 * DIFF_HEADS * 2 * HEAD_DIM)
            for nm in ('diff_lam_q1', 'diff_lam_k1', 'diff_lam_q2', 'diff_lam_k2'):
                p[pre + nm] = 0.1 * jax.random.normal(next(ks), (HEAD_DIM,), jnp.float32)
            p[pre + 'diff_subln'] = _gain(next(ks), DIFF_V_DIM)
            p[pre + 'diff_w_o'] = _dense(next(ks), DIFF_HEADS * DIFF_V_DIM, D_MODEL)
        else:
            p[pre + 'ssm_w_in'] = _dense(next(ks), D_MODEL, SSM_IN_DIM)
            p[pre + 'ssm_conv_w'] = 0.5 * jax.random.normal(next(ks), (SSM_CONV, SSM_CONV_DIM), jnp.float32)
            p[pre + 'ssm_conv_b'] = 0.02 * jax.random.normal(next(ks), (SSM_CONV_DIM,), jnp.float32)
            u = jax.random.uniform(next(ks), (SSM_HEADS,), jnp.float32)
            dt0 = jnp.exp(u * (math.log(DT_MAX) - math.log(DT_MIN)) + math.log(DT_MIN))
            p[pre + 'ssm_dt_bias'] = dt0 + jnp.log(-jnp.expm1(-dt0))
            p[pre + 'ssm_A_log'] = jnp.log(jax.random.uniform(next(ks), (SSM_HEADS,), jnp.float32, 1.0, 16.0))
            p[pre + 'ssm_D'] = 1.0 + 0.1 * jax.random.normal(next(ks), (SSM_HEADS,), jnp.float32)
            p[pre + 'ssm_gate_norm'] = _gain(next(ks), SSM_D_INNER)
            p[pre + 'ssm_w_out'] = _dense(next(ks), SSM_D_INNER, D_MODEL)
        p[pre + 'mlp_norm'] = _gain(next(ks), D_MODEL)
        p[pre + 'mlp_w_up'] = _dense(next(ks), D_MODEL, D_FF)
        p[pre + 'mlp_w_down'] = _dense(next(ks), D_FF, D_MODEL)
    p['final_norm'] = _gain(next(ks), D_MODEL)
    return p


def reference(x, rel_bias,
              l0_mix_norm, l0_dil_w_qkv, l0_dil_w_o, l0_mlp_norm, l0_mlp_w_up, l0_mlp_w_down,
              l1_mix_norm, l1_diff_w_qkv, l1_diff_lam_q1, l1_diff_lam_k1, l1_diff_lam_q2, l1_diff_lam_k2,
              l1_diff_subln, l1_diff_w_o, l1_mlp_norm, l1_mlp_w_up, l1_mlp_w_down,
              l2_mix_norm, l2_ssm_w_in, l2_ssm_conv_w, l2_ssm_conv_b, l2_ssm_dt_bias, l2_ssm_A_log, l2_ssm_D,
              l2_ssm_gate_norm, l2_ssm_w_out, l2_mlp_norm, l2_mlp_w_up, l2_mlp_w_down,
              l3_mix_norm, l3_dil_w_qkv, l3_dil_w_o, l3_mlp_norm, l3_mlp_w_up, l3_mlp_w_down,
              final_norm):
    mix_norms = [l0_mix_norm, l1_mix_norm, l2_mix_norm, l3_mix_norm]
    mixer_args = [
        (l0_dil_w_qkv, l0_dil_w_o),
        (l1_diff_w_qkv, l1_diff_lam_q1, l1_diff_lam_k1, l1_diff_lam_q2, l1_diff_lam_k2, l1_diff_subln, l1_diff_w_o),
        (l2_ssm_w_in, l2_ssm_conv_w, l2_ssm_conv_b, l2_ssm_dt_bias, l2_ssm_A_log, l2_ssm_D, l2_ssm_gate_norm, l2_ssm_w_out),
        (l3_dil_w_qkv, l3_dil_w_o),
    ]
    mlp_args = [
        (l0_mlp_norm, l0_mlp_w_up, l0_mlp_w_down),
        (l1_mlp_norm, l1_mlp_w_up, l1_mlp_w_down),
        (l2_mlp_norm, l2_mlp_w_up, l2_mlp_w_down),
        (l3_mlp_norm, l3_mlp_w_up, l3_mlp_w_down),
    ]
    h = x
    for i in range(DEPTH):
        kind = i % N_MIXERS
        hn = rms_norm(h, mix_norms[i])
        if kind == 0:
            mixed = dilated_mixer(hn, *mixer_args[i], rel_bias)
        elif kind == 1:
            mixed = diff_mixer(hn, *mixer_args[i], rel_bias, diff_lambda_init(i))
        else:
            mixed = ssd_mixer(hn, *mixer_args[i])
        h = h + mixed
        norm_g, w_up, w_down = mlp_args[i]
        h = h + sq_relu_mlp(rms_norm(h, norm_g), w_up, w_down)
    return rms_norm(h, final_norm)
```

```python
import math
import contextlib
import numpy as np
import concourse.bass as bass
import concourse.mybir as mybir
from concourse.bass_utils import run_bass_kernel_spmd

F32 = mybir.dt.float32
BF16 = mybir.dt.bfloat16
AF = mybir.ActivationFunctionType
ALU = mybir.AluOpType
AX = mybir.AxisListType

NT = 2048
EPS = 1e-5
EPOCH = 12000
NEG = -30000.0
PAIRS = [[0, 1], [2, 3], [4, 5], [6, 7]]
DILS = (1, 4, 16)


def lambda_init(layer):
    return 0.8 - 0.6 * math.exp(-0.3 * layer)


class Res:
    __slots__ = ("w", "r")

    def __init__(self):
        self.w = None
        self.r = {}


def mkres(n):
    return [Res() for _ in range(n)]


class Eng:
    def __init__(self, K, name, e, is_pe=False):
        self.K, self.name, self.e, self.is_pe = K, name, e, is_pe
        self.sems = []
        self.semnums = set()
        self.cnt = 0
        self.seen = {}

    def tick(self):
        if not self.sems or self.cnt >= EPOCH:
            s = self.K.new_sem(self.name + str(len(self.sems)))
            self.sems.append(s)
            self.semnums.add(s.num)
            self.cnt = 0
        self.cnt += 1
        s = self.sems[-1]
        self.K.latest[s.num] = (s, self.cnt)
        return (s, self.cnt)

    def wait(self, sem, val):
        if self.seen.get(sem.num, 0) >= val:
            return
        self.e.wait_ge(sem, val)
        self.seen[sem.num] = val


class Bank:
    def __init__(self, t):
        self.t = t
        self.res = Res()


class Kern:
    def __init__(self, nc, es):
        self.nc, self.es = nc, es
        self.latest = {}
        self.nsem = 0
        self.E = {
            "pe": Eng(self, "pe", nc.tensor, True),
            "act": Eng(self, "act", nc.scalar),
            "dve": Eng(self, "dve", nc.vector),
            "pool": Eng(self, "pool", nc.gpsimd),
            "sp": Eng(self, "sp", nc.sync),
        }
        self.dsem = []
        self.dval = []
        self.di = 0
        self.NDMA = 24
        self.banks = []
        self.uid = 0
        self.pending = []

    def new_sem(self, name):
        self.nsem += 1
        return self.es.enter_context(self.nc.semaphore("s_" + name + "_" + str(self.nsem)))

    def _deps(self, eng, reads, writes):
        d = {}

        def add(t):
            s, v = t
            if d.get(s.num, (None, 0))[1] < v:
                d[s.num] = (s, v)

        for r in reads:
            if r.w is not None:
                add(r.w)
        for w in writes:
            if w.w is not None:
                add(w.w)
            for t in w.r.values():
                add(t)
        for s, v in d.values():
            if eng.is_pe and s.num in eng.semnums:
                continue
            eng.wait(s, v)

    def _mark(self, tk, reads, writes):
        for r in reads:
            r.r[tk[0].num] = tk
        for w in writes:
            w.w = tk
            w.r = {}

    def op(self, en, emit, reads=(), writes=()):
        eng = self.E[en]
        self._deps(eng, reads, writes)
        ins = emit(eng.e)
        tk = eng.tick()
        ins.then_inc(tk[0], 1)
        self._mark(tk, reads, writes)

    def dma(self, out, in_, reads=(), writes=(), q="sp"):
        eng = self.E[q]
        self._deps(eng, reads, writes)
        if len(self.dsem) < self.NDMA:
            self.dsem.append(self.new_sem("dma"))
            self.dval.append(0)
        i = self.di % len(self.dsem) if len(self.dsem) == self.NDMA else len(self.dsem) - 1
        self.di += 1
        s = self.dsem[i]
        if self.dval[i] > 0:
            eng.wait(s, self.dval[i])
        ins = eng.e.dma_start(out=out, in_=in_)
        self.dval[i] += 16
        ins.then_inc(s, 16)
        tk = (s, self.dval[i])
        self.latest[s.num] = tk
        self._mark(tk, reads, writes)

    def coll(self, in_t, out_t, reads, writes):
        eng = self.E["pool"]
        self._deps(eng, reads, writes)
        s = self.new_sem("cc")
        cc = eng.e.collective_compute("AllGather", ALU.bypass, ins=[in_t.ap().opt()], outs=[out_t.ap().opt()],
                                      replica_groups=PAIRS)
        cc.then_inc(s)
        tk = (s, 1)
        self.latest[s.num] = tk
        self._mark(tk, reads, writes)

    def barrier(self):
        for eng in self.E.values():
            for s, v in list(self.latest.values()):
                if eng.is_pe and s.num in eng.semnums:
                    continue
                eng.wait(s, v)

    @contextlib.contextmanager
    def phase(self):
        ph = Phase(self)
        with ph.es:
            yield ph
            self.barrier()


class Phase:
    def __init__(self, K):
        self.K = K
        self.es = contextlib.ExitStack()

    def sb(self, name, shape, dt):
        self.K.uid += 1
        return self.es.enter_context(self.K.nc.sbuf_tensor(name + str(self.K.uid), list(shape), dt))


def AP(t, offset, dims):
    return bass.AP(t, offset, [list(d) for d in dims])


def pstep(t):
    n = 1
    for s in list(t.shape)[1:]:
        n *= int(s)
    return n


class WStream:
    NS = 3

    def __init__(self, K, wd):
        self.K, self.wd = K, wd
        self.nu = int(wd.shape[0])
        self.pos = 0
        self.dpos = 0
        self.cpos = 0

    def begin(self, ph, cast="pool"):
        self.cast_eng = cast
        self.st = ph.sb("wst", [128, self.NS, 2048], F32)
        self.bf = ph.sb("wbf", [128, self.NS, 2048], BF16)
        self.str_ = mkres(self.NS)
        self.bfr = mkres(self.NS)
        self.dpos = self.pos
        self.cpos = self.pos
        self._fill()

    def _dma(self):
        if self.dpos >= self.nu:
            return
        i = self.dpos % self.NS
        self.K.dma(self.st[:, i, :], self.wd[self.dpos], reads=[], writes=[self.str_[i]])
        self.dpos += 1
        K = self.K
        K.pend_cnt = getattr(K, "pend_cnt", 0) + 1
        if K.pending and K.pend_cnt >= K.pending[0][0]:
            K.pend_cnt = 0
            K.pending.pop(0)[2]()

    def _cast(self):
        if self.cpos >= self.nu or self.cpos >= self.dpos:
            return
        i = self.cpos % self.NS
        self.K.op(self.cast_eng, lambda e: e.tensor_copy(self.bf[:, i, :], self.st[:, i, :]),
                  reads=[self.str_[i]], writes=[self.bfr[i]])
        self.cpos += 1

    def _fill(self):
        while self.dpos < self.pos + self.NS - 1:
            if self.dpos >= self.nu:
                break
            self._dma()
        while self.cpos < self.pos + 1 and self.cpos < self.dpos:
            self._cast()

    def get(self):
        u = self.pos
        while self.cpos <= u:
            if self.dpos <= self.cpos:
                self._dma()
            self._cast()
        i = u % self.NS
        self.pos += 1
        self._fill()
        return self.bf[:, i, :], self.bfr[i]


class Builder:
    def __init__(self, layers, n_units, skip_mixer=False, skip_mlp=False, stop=None):
        self.stop = stop
        self.layers = layers
        self.skip_mixer, self.skip_mlp = skip_mixer, skip_mlp
        nc = bass.Bass("TRN2", target_bir_lowering=False)
        self.nc = nc
        dt = nc.dram_tensor
        self.xT = dt("xT", [1024, NT], F32, kind="ExternalInput")
        self.wd = dt("wstream", [n_units, 128, 2048], F32, kind="ExternalInput")
        self.cst = dt("cst", [128, CST_W], F32, kind="ExternalInput")
        self.rep = dt("rep", [128, REP_W], F32, kind="ExternalInput")
        self.oh = dt("oh", [33, OH_W], F32, kind="ExternalInput")
        self.relb = dt("relb", [32, 16], F32, kind="ExternalInput")
        self.outT = dt("outT", [1024, NT], F32, kind="ExternalOutput")
        self.kvd_in = [dt("kvd_in%d" % i, [512, 2048], BF16) for i in range(4)]
        self.kvd_out = [dt("kvd_out%d" % i, [1024, 2048], BF16) for i in range(4)]
        self.kvl_in = [dt("kvl_in%d" % i, [512, 2048], BF16) for i in range(12)]
        self.kvl_out = [dt("kvl_out%d" % i, [1024, 2048], BF16) for i in range(12)]
        self.q_loc = dt("q_loc", [3072, 2048], BF16)
        self.rep_dil = dt("rep_dil", [16 * 6 * 128, 256], F32)
        self.rep_dif = dt("rep_dif", [16 * 128, 4608], F32)
        self.tvd = dt("tvd", [16, OH_W], F32)
        self.yT_d = dt("yT_d", [2048, 2048], BF16)
        self.tail_in = dt("tail_in", [128, 128], F32)
        self.tail_out = dt("tail_out", [256, 128], F32)
        self.st_in = dt("st_in", [128, 2048], F32)
        self.st_out = dt("st_out", [256, 2048], F32)
        self.xbc_raw = dt("xbc_raw", [4096, 2048], F32)
        self.xbc_act = dt("xbc_act", [4096, 2048], F32)
        self.zs = dt("zs", [2048, 2048], F32)
        self.dres = {}

    def dr(self, name):
        if name not in self.dres:
            self.dres[name] = Res()
        return self.dres[name]

    def build(self):
        nc = self.nc
        with contextlib.ExitStack() as es:
            K = Kern(nc, es)
            self.K = K
            for i in range(8):
                t = es.enter_context(nc.psum_tensor("ps%d" % i, [128, 512], F32))
                K.banks.append(Bank(t))
            sb = lambda n, s, d: es.enter_context(nc.sbuf_tensor(n, list(s), d))
            self.hT = sb("hT", [128, 8, NT], F32)
            self.hres = mkres(4)
            self.c = sb("cst_sb", [128, CST_W], F32)
            self.cres = Res()
            self.onesb = sb("onesb", [128, 128], BF16)
            self.flagb = sb("flagb", [128, 128], BF16)
            self.identb_res = Res()
            self.misc = sb("misc", [128, 16], F32)
            self.miscres = Res()
            K.op("dve", lambda e: e.memset(self.misc[:], 0.0), writes=[self.miscres])
            self.ws = WStream(K, self.wd.ap())
            K.dma(self.c[:], self.cst.ap(), writes=[self.cres])
            for T in range(4):
                K.dma(self.hT[:, :, T * 512:(T + 1) * 512],
                      AP(self.xT, T * 512, [[NT, 128], [128 * NT, 8], [1, 512]]), writes=[self.hres[T]])
            K.op("dve", lambda e: e.tensor_copy(self.onesb[:], self.c[:, C_ONES:C_ONES + 128]),
                 reads=[self.cres], writes=[self.identb_res])
            K.op("dve", lambda e: e.tensor_scalar(out=self.flagb[:], in0=self.c[:, C_ONES:C_ONES + 128],
                                                  scalar1=self.c[:, C_FLAG:C_FLAG + 1], scalar2=None, op0=ALU.mult),
                 reads=[self.cres], writes=[self.identb_res])
            kinds = [l % 3 for l in self.layers]
            if not self.skip_mixer:
                self.setup_bias(0 in kinds, 1 in kinds)
            for l in self.layers:
                kind = l % 3
                if self.skip_mixer:
                    pass
                elif kind == 0:
                    self.dilated_layer(l)
                elif kind == 1:
                    self.diff_layer(l)
                else:
                    self.ssd_layer(l)
                if not self.skip_mlp:
                    self.mlp(l)
            self.final_norm()
            K.barrier()
        return nc

    def ones(self):
        return self.c[:, C_ONES:C_ONES + 128]

    def ident(self):
        return self.c[:, C_IDENT:C_IDENT + 128]

    def triU(self):
        return self.c[:, C_TRIU:C_TRIU + 128]

    def maskL(self):
        return self.c[:, C_MASKL:C_MASKL + 128]

    def gain(self, i, c):
        o = C_GAIN + i * 8 + c
        return self.c[:, o:o + 1]

    def flag(self):
        return self.c[:, C_FLAG:C_FLAG + 1]

    def setup_bias(self, need_dil, need_dif):
        if not (need_dil or need_dif):
            return
        K = self.K
        with K.phase() as ph:
            rb = ph.sb("rb", [33, 16], F32)
            rbr = Res()
            K.op("dve", lambda e: e.memset(rb[32:33, :], NEG), writes=[rbr])
            K.dma(rb[0:32, :], self.relb.ap(), writes=[rbr])
            ohs = ph.sb("ohs", [33, OH_W], F32)
            ohr = Res()
            K.dma(ohs[:], self.oh.ap(), writes=[ohr])
            tv = ph.sb("tv", [16, OH_W], F32)
            tvr = Res()
            nb = OH_W // 512
            for j in range(nb):
                bk = K.banks[j % 2]
                K.op("pe", lambda e: e.matmul(bk.t[0:16, :], lhsT=rb[:, :], rhs=ohs[:, j * 512:(j + 1) * 512],
                                              start=True, stop=True), reads=[rbr, ohr], writes=[bk.res])
                K.op("act", lambda e: e.copy(tv[:, j * 512:(j + 1) * 512], bk.t[0:16, :]), reads=[bk.res], writes=[tvr])
            K.dma(self.tvd.ap(), tv[:], reads=[tvr], writes=[self.dr("tvd")])
        def mk_dil(h):
            return lambda: K.dma(AP(self.rep_dil, h * 6 * 128 * 256, [[128 * 256, 6], [256, 128], [1, 256]]),
                                 AP(self.tvd, h * OH_W, [[256, 6], [0, 128], [1, 256]]),
                                 reads=[self.dr("tvd")], writes=[self.dr("rep_dil")])

        def mk_dif(h):
            return lambda: K.dma(AP(self.rep_dif, h * 128 * 4608, [[4608, 128], [1, 4608]]),
                                 AP(self.tvd, h * OH_W + 1536, [[0, 128], [1, 4608]]),
                                 reads=[self.dr("tvd")], writes=[self.dr("rep_dif")])

        if need_dil:
            K.pending += [(2, "dil", mk_dil(h)) for h in range(16)]
        if need_dif:
            K.pending += [(5, "dif", mk_dif(h)) for h in range(16)]

    def flush_pending(self, tag):
        keep = []
        for st, tg, fn in self.K.pending:
            if tg == tag:
                fn()
            else:
                keep.append((st, tg, fn))
        self.K.pending[:] = keep

    def rmsnorm(self, ph, gi, dst_fn, dst_res, src=None, srcres=None):
        K = self.K
        src = self.hT if src is None else src
        srcres = self.hres if srcres is None else srcres
        sq = ph.sb("sq", [128, 2, 512], F32)
        sqr = mkres(2)
        lnv = ph.sb("lnv", [128, 512], F32)
        rstd = ph.sb("rstd", [128, 512], F32)
        lnr, rsr = Res(), Res()
        bank = K.banks[7]
        for T in range(4):
            sl = slice(T * 512, (T + 1) * 512)
            for c in range(8):
                i = c % 2
                if c % 2 == 0:
                    K.op("pool", lambda e: e.tensor_tensor(out=sq[:, i, :], in0=src[:, c, sl], in1=src[:, c, sl], op=ALU.mult),
                         reads=[srcres[T]], writes=[sqr[i]])
                else:
                    K.op("act", lambda e: e.activation(out=sq[:, i, :], in_=src[:, c, sl], func=AF.Square), reads=[srcres[T]], writes=[sqr[i]])
                K.op("pe", lambda e: e.matmul(bank.t[:, :], lhsT=self.ones(), rhs=sq[:, i, :], start=(c == 0), stop=(c == 7)),
                     reads=[sqr[i], self.cres], writes=[bank.res])
            K.op("act", lambda e: e.activation(out=lnv[:], in_=bank.t[:, :], func=AF.Ln, scale=1.0 / 1024, bias=EPS),
                 reads=[bank.res], writes=[lnr])
            K.op("act", lambda e: e.activation(out=rstd[:], in_=lnv[:], func=AF.Exp, scale=-0.5), reads=[lnr], writes=[rsr])
            for c in range(8):
                K.op("dve", lambda e: e.scalar_tensor_tensor(out=dst_fn(c, T), in0=src[:, c, sl], scalar=self.gain(gi, c),
                                                             in1=rstd[:], op0=ALU.mult, op1=ALU.mult),
                     reads=[srcres[T], rsr, self.cres], writes=[dst_res[T]])

    def proj_F(self, src_fn, n_oc, KC, evac, rot=(0, 1, 2, 3), N=512, ntile=4):
        K = self.K
        per = 2048 // (KC * 128)
        wv = None
        bi = 0
        for oc in range(n_oc):
            s = oc % per
            if s == 0:
                w, wres = self.ws.get()
                wv = w.rearrange("p (s k j) -> p s k j", s=per, k=KC)
            for T in range(ntile):
                bank = K.banks[rot[bi % len(rot)]]
                bi += 1
                for k in range(KC):
                    ap, r = src_fn(k, T)
                    K.op("pe", lambda e: e.matmul(bank.t[:, 0:N], lhsT=wv[:, s, k, :], rhs=ap, start=(k == 0), stop=(k == KC - 1)),
                         reads=[wres, r], writes=[bank.res])
                evac(oc, T, bank)

    def proj_T(self, src_fn, n_pairs, evac, rot=(0, 1, 2, 3), ntile=16):
        K = self.K
        bi = 0
        for un in range(n_pairs):
            wa, ra = self.ws.get()
            wb, rb = self.ws.get()
            wva = wa.rearrange("p (k n) -> p k n", k=4)
            wvb = wb.rearrange("p (k n) -> p k n", k=4)
            for tt in range(ntile):
                bank = K.banks[rot[bi % len(rot)]]
                bi += 1
                for k in range(8):
                    ap, r = src_fn(k, tt)
                    wv, wr = (wva, ra) if k < 4 else (wvb, rb)
                    K.op("pe", lambda e: e.matmul(bank.t[:, :], lhsT=ap, rhs=wv[:, k % 4, :], start=(k == 0), stop=(k == 7)),
                         reads=[wr, r], writes=[bank.res])
                evac(un, tt, bank)

    def add_to_h(self, oc, T, bank):
        sl = slice(T * 512, (T + 1) * 512)
        self.K.op("dve", lambda e: e.tensor_tensor(out=self.hT[:, oc, sl], in0=bank.t[:, :], in1=self.hT[:, oc, sl], op=ALU.add),
                  reads=[bank.res, self.hres[T]], writes=[self.hres[T]])

    def mlp(self, l):
        K = self.K
        with K.phase() as ph:
            self.ws.begin(ph)
            hn = ph.sb("hn", [128, 8, NT], BF16)
            hnr = mkres(4)
            self.rmsnorm(ph, 2 * l + 1, lambda c, T: hn[:, c, T * 512:(T + 1) * 512], hnr)
            u = ph.sb("u", [128, 8, NT], BF16)
            ur = [mkres(4) for _ in range(8)]
            rl = ph.sb("rl", [128, 2, 512], F32)
            rlr = mkres(2)
            cnt = [0]

            def evac_up(oc, T, bank):
                i = cnt[0] % 2
                cnt[0] += 1
                sl = slice(T * 512, (T + 1) * 512)
                K.op("act", lambda e: e.activation(out=rl[:, i, :], in_=bank.t[:, :], func=AF.Relu), reads=[bank.res], writes=[rlr[i]])
                K.op("dve", lambda e: e.tensor_tensor(out=u[:, oc, sl], in0=rl[:, i, :], in1=rl[:, i, :], op=ALU.mult),
                     reads=[rlr[i]], writes=[ur[oc][T]])

            for G in range(4):
                self.proj_F(lambda k, T: (hn[:, k, T * 512:(T + 1) * 512], hnr[T]), 8, 8, evac_up, rot=(0, 1, 2))
                self.proj_F(lambda k, T: (u[:, k, T * 512:(T + 1) * 512], ur[k][T]), 8, 8, self.add_to_h, rot=(3, 4, 5))

    def final_norm(self):
        K = self.K
        with K.phase() as ph:
            o = ph.sb("fo", [128, 8, NT], F32)
            orr = mkres(4)
            self.rmsnorm(ph, 8, lambda c, T: o[:, c, T * 512:(T + 1) * 512], orr)
            for T in range(4):
                K.dma(AP(self.outT, T * 512, [[NT, 128], [128 * NT, 8], [1, 512]]), o[:, :, T * 512:(T + 1) * 512],
                      reads=[orr[T]], writes=[self.dr("outT")])

    def diff_layer(self, l):
        K = self.K
        li = lambda_init(l)
        with K.phase() as ph:
            qT = ph.sb("qT", [128, 8, NT], BF16)
            qr = [mkres(4) for _ in range(8)]
            with K.phase() as p1:
                self.ws.begin(p1, cast="dve")
                hn = p1.sb("hn", [128, 8, NT], BF16)
                hnr = mkres(4)
                self.rmsnorm(p1, 2 * l, lambda c, T: hn[:, c, T * 512:(T + 1) * 512], hnr)
                src = lambda k, T: (hn[:, k, T * 512:(T + 1) * 512], hnr[T])

                vst = p1.sb("vst", [128, 16, 512], BF16)
                vstr = Res()

                def evac_v(un, tt, bank):
                    K.op("act", lambda e: e.copy(vst[:, tt, :], bank.t[:, :]), reads=[bank.res], writes=[vstr])
                    if tt == 15:
                        for hf in range(2):
                            K.dma(AP(self.kvd_in[2 + hf], un * 512, [[1024, 128], [128 * 1024, 8], [1, 512]]), vst[:, hf * 8:hf * 8 + 8, :], reads=[vstr],
                                  writes=[self.dr("kvd_in%d" % (2 + hf))])

                self.proj_T(lambda k, tt: (hn[:, k, tt * 128:(tt + 1) * 128], hnr[tt // 4]), 2, evac_v)
                for i in (2, 3):
                    K.coll(self.kvd_in[i], self.kvd_out[i], reads=[self.dr("kvd_in%d" % i)], writes=[self.dr("kvd_out%d" % i)])
                kst = p1.sb("kst", [128, 2, NT], BF16)
                kstr = mkres(2)

                def evac_k(oc, T, bank):
                    i = oc % 2
                    K.op("act", lambda e: e.copy(kst[:, i, T * 512:(T + 1) * 512], bank.t[:, :]), reads=[bank.res], writes=[kstr[i]])
                    if T == 3:
                        K.dma(AP(self.kvd_in[oc // 4], (oc % 4) * 128 * 2048, [[2048, 128], [1, 2048]]), kst[:, i, :], reads=[kstr[i]],
                              writes=[self.dr("kvd_in%d" % (oc // 4))])
                        if oc % 4 == 3:
                            K.coll(self.kvd_in[oc // 4], self.kvd_out[oc // 4], reads=[self.dr("kvd_in%d" % (oc // 4))], writes=[self.dr("kvd_out%d" % (oc // 4))])

                self.proj_F(src, 8, 8, evac_k)

                def evac_q(oc, T, bank):
                    K.op("act", lambda e: e.activation(out=qT[:, oc, T * 512:(T + 1) * 512], in_=bank.t[:, :], func=AF.Copy, scale=0.125),
                         reads=[bank.res], writes=[qr[oc][T]])

                self.proj_F(src, 8, 8, evac_q)
            self.flush_pending("dif")
            if self.stop == "D1":
                return
            with K.phase() as p2:
                lv = p2.sb("lv", [128, 256], F32)
                lvr = Res()
                K.dma(lv[:], AP(self.rep, R_LAM, [[REP_W, 128], [1, 256]]), writes=[lvr])
                lt = p2.sb("lt", [128, 128], F32)
                ls = p2.sb("ls", [128, 8], F32)
                lsr = Res()
                K.op("dve", lambda e: e.tensor_tensor(out=lt[:, 0:64], in0=lv[:, 0:64], in1=lv[:, 64:128], op=ALU.mult), reads=[lvr], writes=[lsr])
                K.op("dve", lambda e: e.tensor_tensor(out=lt[:, 64:128], in0=lv[:, 128:192], in1=lv[:, 192:256], op=ALU.mult), reads=[lvr, lsr], writes=[lsr])
                K.op("dve", lambda e: e.reduce_sum(out=ls[:, 0:1], in_=lt[:, 0:64], axis=AX.X), reads=[lsr], writes=[lsr])
                K.op("dve", lambda e: e.reduce_sum(out=ls[:, 1:2], in_=lt[:, 64:128], axis=AX.X), reads=[lsr], writes=[lsr])
                K.op("act", lambda e: e.activation(out=ls[:, 2:4], in_=ls[:, 0:2], func=AF.Exp), reads=[lsr], writes=[lsr])
                K.op("dve", lambda e: e.tensor_tensor(out=ls[:, 4:5], in0=ls[:, 3:4], in1=ls[:, 2:3], op=ALU.subtract), reads=[lsr], writes=[lsr])
                K.op("dve", lambda e: e.tensor_scalar(out=ls[:, 5:6], in0=ls[:, 4:5], scalar1=-li, scalar2=None, op0=ALU.add), reads=[lsr], writes=[lsr])
                K.op("dve", lambda e: e.tensor_scalar(out=ls[:, 6:7], in0=self.c[:, C_SUBLN:C_SUBLN + 1], scalar1=1.0 - li, scalar2=None, op0=ALU.mult),
                     reads=[lsr, self.cres], writes=[lsr])
                neglam = ls[:, 5:6]
                sg = ls[:, 6:7]

                kb = p2.sb("kb", [128, 2, 2, NT], BF16)
                vb = p2.sb("vb", [128, 2, 2, 16, 128], BF16)
                kbr = mkres(2)
                vbr = mkres(2)
                SW = 2432
                strip = p2.sb("strip", [128, 2, 2, SW], F32)
                stripr = [mkres(2), mkres(2)]
                NTB, NPB, LA = 3, 6, 4
                tmp = p2.sb("tmp", [128, NTB, 512], F32)
                tmpr = mkres(NTB)
                pt = p2.sb("pt", [128, NPB, 512], BF16)
                ptr = mkres(NPB)
                rc = p2.sb("rc", [128, 2, 512], F32)
                rcr = mkres(2)
                of = p2.sb("of", [128, 512], F32)
                ofr = Res()
                osq = p2.sb("osq", [128, 512], F32)
                osr = Res()
                qz2 = p2.sb("qz2", [128, 2, 2, NT], BF16)
                qzr = mkres(2)
                K.op("pool", lambda e: e.memset(qz2[64:128, :, 0, :], 0.0), writes=qzr)
                K.op("pool", lambda e: e.memset(qz2[0:64, :, 1, :], 0.0), writes=qzr)
                jobs = []
                for h in range(8):
                    for qc in range(4):
                        tiles = [(0, kt) for kt in range(16)] + [(1, kt) for kt in range(4 * qc + 4)]
                        for ti, (slot, kt) in enumerate(tiles):
                            for m in range(2):
                                jobs.append((h, qc, m, slot, kt, ti == 0, ti == len(tiles) - 1))
                deferred = []

                def load_head(h):
                    b = h % 2
                    K.dma(kb[:, b, 0, :], AP(self.kvd_out[h // 4], (h % 4) * 128 * 2048, [[2048, 128], [1, 2048]]), reads=[self.dr("kvd_out%d" % (h // 4))], writes=[kbr[b]])
                    K.dma(kb[:, b, 1, :], AP(self.kvd_in[h // 4], (h % 4) * 128 * 2048, [[2048, 128], [1, 2048]]), reads=[self.dr("kvd_in%d" % (h // 4))], writes=[kbr[b]])
                    for hf in range(2):
                        K.dma(vb[:, b, 0, hf * 8:hf * 8 + 8, :], AP(self.kvd_out[2 + hf], h * 128, [[1024, 128], [128 * 1024, 8], [1, 128]]),
                              reads=[self.dr("kvd_out%d" % (2 + hf))], writes=[vbr[b]])
                        K.dma(vb[:, b, 1, hf * 8:hf * 8 + 8, :], AP(self.kvd_in[2 + hf], h * 128, [[1024, 128], [128 * 1024, 8], [1, 128]]),
                              reads=[self.dr("kvd_in%d" % (2 + hf))], writes=[vbr[b]])
                    K.op("dve", lambda e: e.tensor_scalar(out=vb[:, b, 0, :, :], in0=vb[:, b, 0, :, :], scalar1=self.flag(), scalar2=None, op0=ALU.mult),
                         reads=[vbr[b], self.cres], writes=[vbr[b]])
                    K.op("pool", lambda e: e.tensor_copy(qz2[0:64, b, 0, :], qT[0:64, h, :]), reads=qr[h], writes=[qzr[b]])
                    K.op("pool", lambda e: e.tensor_copy(qz2[64:128, b, 1, :], qT[64:128, h, :]), reads=qr[h], writes=[qzr[b]])

                def load_strips(h):
                    for m in range(2):
                        col = m * 8 + h
                        K.dma(strip[:, h % 2, m, :], AP(self.rep_dif, col * 128 * 4608 + 127, [[4607, 128], [1, SW]]),
                              reads=[self.dr("rep_dif")], writes=[stripr[h % 2][m]])

                def emit_score(idx):
                    h, qc, m, slot, kt, first, last = jobs[idx]
                    b = h % 2
                    rows = slice(m * 64, (m + 1) * 64)
                    qsl = slice(qc * 512, (qc + 1) * 512)
                    G = (2048 if slot == 0 else 0) + qc * 512 - kt * 128
                    y0 = G + 384
                    sb_ = K.banks[idx % 3]
                    ip = idx % NPB
                    K.op("pe", lambda e: e.matmul(sb_.t[:, :], lhsT=kb[:, b, slot, kt * 128:(kt + 1) * 128], rhs=qz2[:, b, m, qsl],
                                                  start=True, stop=True), reads=[kbr[b], qzr[b]], writes=[sb_.res])
                    if G - 127 >= 1512:
                        K.op("act", lambda e: e.activation(out=pt[:, ip, :], in_=sb_.t[:, :], func=AF.Exp, bias=strip[:, b, m, SW - 1:SW], scale=1.0),
                             reads=[sb_.res, stripr[b][m]], writes=[ptr[ip]])
                    else:
                        it_ = idx % NTB
                        K.op("dve", lambda e: e.scalar_tensor_tensor(out=tmp[:, it_, :], in0=sb_.t[:, :], scalar=60.0, in1=strip[:, b, m, y0:y0 + 512],
                                                                     op0=ALU.min, op1=ALU.add), reads=[sb_.res, stripr[b][m]], writes=[tmpr[it_]])
                        K.op("act", lambda e: e.activation(out=pt[:, ip, :], in_=tmp[:, it_, :], func=AF.Exp), reads=[tmpr[it_]], writes=[ptr[ip]])

                def epi2(h, qc):
                    qsl = slice(qc * 512, (qc + 1) * 512)
                    sbk = K.banks[7]
                    K.op("pe", lambda e: e.matmul(sbk.t[:, :], lhsT=self.ones(), rhs=osq[:], start=True, stop=True), reads=[osr, self.cres], writes=[sbk.res])
                    K.op("act", lambda e: e.activation(out=osq[:], in_=sbk.t[:, :], func=AF.Ln, scale=1.0 / 128, bias=EPS), reads=[sbk.res], writes=[osr])
                    K.op("act", lambda e: e.activation(out=osq[:], in_=osq[:], func=AF.Exp, scale=-0.5), reads=[osr], writes=[osr])
                    K.op("dve", lambda e: e.scalar_tensor_tensor(out=qT[:, h, qsl], in0=of[:], scalar=sg, in1=osq[:], op0=ALU.mult, op1=ALU.mult),
                         reads=[ofr, osr, lsr], writes=[qr[h][qc]])

                def emit_pv(idx):
                    h, qc, m, slot, kt, first, last = jobs[idx]
                    b = h % 2
                    ip = idx % NPB
                    numb = K.banks[3 + 2 * m]
                    denb = K.banks[4 + 2 * m]
                    K.op("pe", lambda e: e.matmul(numb.t[:, :], lhsT=vb[:, b, slot, kt, :], rhs=pt[:, ip, :], start=first, stop=last),
                         reads=[vbr[b], ptr[ip]], writes=[numb.res])
                    dl = self.flagb if slot == 0 else self.onesb
                    K.op("pe", lambda e: e.matmul(denb.t[:, :], lhsT=dl[:], rhs=pt[:, ip, :], start=first, stop=last),
                         reads=[self.identb_res, ptr[ip]], writes=[denb.res])
                    if last:
                        K.op("act", lambda e: e.activation(out=rc[:, m, :], in_=denb.t[:, :], func=AF.Ln), reads=[denb.res], writes=[rcr[m]])
                        K.op("act", lambda e: e.activation(out=rc[:, m, :], in_=rc[:, m, :], func=AF.Exp, scale=-1.0), reads=[rcr[m]], writes=[rcr[m]])
                        K.op("dve", lambda e: e.tensor_tensor(out=rc[:, m, :], in0=numb.t[:, :], in1=rc[:, m, :], op=ALU.mult),
                             reads=[numb.res, rcr[m]], writes=[rcr[m]])
                        if m == 1:
                            K.op("dve", lambda e: e.scalar_tensor_tensor(out=of[:], in0=rc[:, 1, :], scalar=neglam, in1=rc[:, 0, :], op0=ALU.mult, op1=ALU.add),
                                 reads=[rcr[0], rcr[1], lsr], writes=[ofr])
                            K.op("pool", lambda e: e.tensor_tensor(out=osq[:], in0=of[:], in1=of[:], op=ALU.mult), reads=[ofr], writes=[osr])
                            deferred.append((idx + LA + 10, lambda: epi2(h, qc)))

                nj = len(jobs)
                for h0 in range(2):
                    load_head(h0)
                    load_strips(h0)
                for idx in range(nj + LA):
                    if idx < nj:
                        emit_score(idx)
                    if idx - LA >= 0:
                        emit_pv(idx - LA)
                        jh = jobs[idx - LA]
                        if jh[1] == 3 and jh[2] == 1 and jh[6] and jh[0] + 2 < 8:
                            load_head(jh[0] + 2)
                            load_strips(jh[0] + 2)
                    while deferred and deferred[0][0] <= idx:
                        deferred.pop(0)[1]()
                while deferred:
                    deferred.pop(0)[1]()
            with K.phase() as p3:
                self.ws.begin(p3)
                self.proj_F(lambda k, T: (qT[:, k, T * 512:(T + 1) * 512], qr[k][T]), 8, 8, self.add_to_h)

    def regroup_ap(self, t, k, D, T):
        base = k * NT
        ps = pstep(t.tensor if hasattr(t, "tensor") else t)
        th = t.tensor if hasattr(t, "tensor") else t
        if D == 1:
            return AP(th, base + 512 * T, [[ps, 128], [1, 512]])
        if D == 4:
            return AP(th, base + T, [[ps, 128], [4, 512]])
        return AP(th, base + 4 * T, [[ps, 128], [1, 4], [16, 128]])

    def dilated_layer(self, l):
        K = self.K
        with K.phase() as ph:
            with K.phase() as p1:
                self.ws.begin(p1, cast="dve")
                hn = p1.sb("hn", [128, 8, NT], BF16)
                hnr = mkres(4)
                allr = Res()
                self.rmsnorm(p1, 2 * l, lambda c, T: hn[:, c, T * 512:(T + 1) * 512], hnr)
                K.op("pool", lambda e: e.tensor_copy(self.misc[:, 0:1], self.misc[:, 1:2]), reads=hnr + [self.miscres], writes=[allr, self.miscres])
                st = p1.sb("qkst", [128, 2, NT], BF16)
                str_ = mkres(2)
                vst = p1.sb("vst", [128, 4, 512], BF16)
                vstr4 = mkres(4)
                hps = pstep(hn)
                for g, D in enumerate(DILS):
                    L = NT // D
                    src = lambda k, T: (self.regroup_ap(hn, k, D, T), allr)

                    def qk_proj(which, scale):
                        def evac_qk(oc, T, bank):
                            i = oc % 2
                            if D == 1:
                                o_ap, i_ap = st[:, i, T * 512:(T + 1) * 512], bank.t[:, :]
                            elif D == 4:
                                o_ap = AP(st, i * NT + 128 * T, [[2 * NT, 128], [512, 4], [1, 128]])
                                i_ap = bank.t[:, :].rearrange("p (a r) -> p r a", r=4)
                            else:
                                o_ap = AP(st, i * NT + 32 * T, [[2 * NT, 128], [128, 16], [1, 32]])
                                i_ap = bank.t[:, :].rearrange("p (a r) -> p r a", r=16)
                            K.op("act", lambda e: e.activation(out=o_ap, in_=i_ap, func=AF.Copy, scale=scale),
                                 reads=[bank.res], writes=[str_[i]])
                            if T == 3:
                                if which == 0:
                                    K.dma(AP(self.q_loc, (g * 1024 + oc * 128) * 2048, [[2048, 128], [1, 2048]]), st[:, i, :], reads=[str_[i]],
                                          writes=[self.dr("q_loc")])
                                else:
                                    ch = g * 4 + oc // 4
                                    K.dma(AP(self.kvl_in[ch], (oc % 4) * 128 * 2048, [[2048, 128], [1, 2048]]), st[:, i, :], reads=[str_[i]],
                                          writes=[self.dr("kvl_in%d" % ch)])
                                    if oc % 4 == 3:
                                        K.coll(self.kvl_in[ch], self.kvl_out[ch], reads=[self.dr("kvl_in%d" % ch)], writes=[self.dr("kvl_out%d" % ch)])

                        self.proj_F(lambda k, T: (hn[:, k, T * 512:(T + 1) * 512], hnr[T]), 8, 8, evac_qk)

                    def vsrc(k, tt):
                        return hn[:, k, tt * 128:(tt + 1) * 128], hnr[tt // 4]

                    def evac_v(un, tt, bank):
                        i = tt % 4
                        K.op("act", lambda e: e.copy(vst[:, i, :], bank.t[:, :]), reads=[bank.res], writes=[vstr4[i]])
                        ch = g * 4 + 2 + un
                        dst = AP(self.kvl_in[ch], (128 * tt // D) * 512, [[512, 128 // D], [L * 512, D], [1, 512]])
                        K.dma(dst, vst[:, i, :], reads=[vstr4[i]], writes=[self.dr("kvl_in%d" % ch)])

                    qk_proj(1, 1.0)
                    self.proj_T(vsrc, 2, evac_v)
                    for hf in range(2):
                        ch = g * 4 + 2 + hf
                        K.coll(self.kvl_in[ch], self.kvl_out[ch], reads=[self.dr("kvl_in%d" % ch)], writes=[self.dr("kvl_out%d" % ch)])
                    qk_proj(0, 0.125)
            self.flush_pending("dil")
            oT = ph.sb("oT", [128, 8, NT], BF16)
            otr = [mkres(4) for _ in range(8)]
            with K.phase() as p2:
                qz = p2.sb("qz", [128, 2, 2, NT], BF16)
                ko = p2.sb("ko", [128, 2, NT], BF16)
                kh = p2.sb("kh", [128, 2, 16, 128], BF16)
                vo = p2.sb("vo", [128, 1, 16, 128], BF16)
                vh = p2.sb("vh", [128, 1, 16, 128], BF16)
                voz = p2.sb("voz", [128, 2, 2, 16, 128], BF16)
                vhz = p2.sb("vhz", [128, 2, 2, 16, 128], BF16)
                oz = p2.sb("oz", [128, 2, 128], BF16)
                fz = p2.sb("fz", [128, 2, 128], BF16)
                ozr = Res()
                bt = p2.sb("bt", [128, 2, 2, 2, 128], F32)
                ldr = mkres(2)
                vzr = mkres(2)
                vldr = Res()
                K.op("pool", lambda e: e.memset(qz[64:128, :, 0, :], 0.0), writes=ldr)
                K.op("pool", lambda e: e.memset(qz[0:64, :, 1, :], 0.0), writes=ldr)
                K.op("pool", lambda e: e.memset(voz[:], 0.0), writes=vzr)
                K.op("pool", lambda e: e.memset(vhz[:], 0.0), writes=vzr)
                K.op("dve", lambda e: e.memset(oz[:], 0.0), writes=[ozr])
                K.op("dve", lambda e: e.memset(fz[:], 0.0), writes=[ozr])
                for hh_ in range(2):
                    K.op("dve", lambda e: e.tensor_copy(oz[:, hh_, hh_ * 64:hh_ * 64 + 64], self.onesb[:, 0:64]), reads=[self.identb_res, ozr], writes=[ozr])
                    K.op("dve", lambda e: e.tensor_copy(fz[:, hh_, hh_ * 64:hh_ * 64 + 64], self.flagb[:, 0:64]), reads=[self.identb_res, ozr], writes=[ozr])
                accn = p2.sb("accn", [128, NT], F32)
                accd = p2.sb("accd", [128, NT], F32)
                accr = Res()
                NTB, NPB, LA = 3, 6, 4
                tmp = p2.sb("tmp", [128, NTB, 512], F32)
                tmpr = mkres(NTB)
                pt = p2.sb("pt", [128, NPB, 512], BF16)
                ptr = mkres(NPB)
                combos = [(p, g) for p in range(8) for g in range(3)]
                jobs = [(ci, qgp, hh, half) for ci in range(len(combos)) for qgp in range(4) for hh in range(2) for half in range(2)]
                accp = [accr]

                def load_pg(ci):
                    p, g = combos[ci]
                    D = DILS[g]
                    b = ci % 2
                    L = NT // D
                    bpr = 16 // D
                    kc_in, kc_out = self.kvl_in[g * 4 + p // 4], self.kvl_out[g * 4 + p // 4]
                    krow = (p % 4) * 128 * 2048
                    K.dma(qz[0:64, b, 0, :], AP(self.q_loc, (g * 1024 + p * 128) * 2048, [[2048, 64], [1, 2048]]), reads=[self.dr("q_loc")], writes=[ldr[b]])
                    K.dma(qz[64:128, b, 1, :], AP(self.q_loc, (g * 1024 + p * 128 + 64) * 2048, [[2048, 64], [1, 2048]]), reads=[self.dr("q_loc")], writes=[ldr[b]])
                    K.dma(ko[:, b, :], AP(kc_in, krow, [[2048, 128], [1, 2048]]), reads=[self.dr("kvl_in%d" % (g * 4 + p // 4))], writes=[ldr[b]])
                    K.dma(kh[:, b, 0:D, :], AP(kc_out, krow + L - 128, [[2048, 128], [L, D], [1, 128]]),
                          reads=[self.dr("kvl_out%d" % (g * 4 + p // 4))], writes=[ldr[b]])
                    vch = g * 4 + 2 + p // 4
                    vcol = (p % 4) * 128
                    K.dma(vo[:, 0, :, :], AP(self.kvl_in[vch], vcol, [[512, 128], [128 * 512, 16], [1, 128]]),
                          reads=[self.dr("kvl_in%d" % vch)], writes=[vldr])
                    K.dma(vh[:, 0, 0:D, :], AP(self.kvl_out[vch], vcol + (bpr - 1) * 128 * 512, [[512, 128], [bpr * 128 * 512, D], [1, 128]]),
                          reads=[self.dr("kvl_out%d" % vch)], writes=[vldr])
                    for hh_ in range(2):
                        K.dma(bt[:, b, hh_, :, :], AP(self.rep_dil, ((2 * p + hh_) * 6 + 2 * g) * 128 * 256 + 127, [[255, 128], [128 * 256, 2], [1, 128]]),
                              reads=[self.dr("rep_dil")], writes=[ldr[b]])
                    K.op("dve", lambda e: e.tensor_scalar(out=vh[:, 0, 0:D, :], in0=vh[:, 0, 0:D, :], scalar1=self.flag(), scalar2=None, op0=ALU.mult),
                         reads=[vldr, self.cres], writes=[vldr])
                    for hh_ in range(2):
                        fs = slice(hh_ * 64, hh_ * 64 + 64)
                        K.op("pool", lambda e: e.tensor_copy(voz[:, b, hh_, :, fs], vo[:, 0, :, fs]), reads=[vldr], writes=[vzr[b]])
                        K.op("pool", lambda e: e.tensor_copy(vhz[:, b, hh_, 0:D, fs], vh[:, 0, 0:D, fs]), reads=[vldr], writes=[vzr[b]])

                def blocks_of(idx):
                    ci, qgp, hh, half = jobs[idx]
                    p, g = combos[ci]
                    D = DILS[g]
                    b = ci % 2
                    bpr = 16 // D
                    rows = slice(hh * 64, (hh + 1) * 64)
                    out = []
                    for j in range(2):
                        blk = qgp * 4 + half * 2 + j
                        if blk % bpr == 0:
                            kprev = kh[:, b, blk // bpr, :]
                            vprev = vhz[:, b, hh, blk // bpr, :]
                            dprev = fz[:, hh, :]
                        else:
                            kprev = ko[:, b, (blk - 1) * 128:blk * 128]
                            vprev = voz[:, b, hh, blk - 1, :]
                            dprev = oz[:, hh, :]
                        kcur = ko[:, b, blk * 128:(blk + 1) * 128]
                        vcur = voz[:, b, hh, blk, :]
                        out.append((blk, kprev, kcur, vprev, dprev, vcur))
                    return out

                def emit_score(idx):
                    ci, qgp, hh, half = jobs[idx]
                    b = ci % 2
                    rows = slice(hh * 64, (hh + 1) * 64)
                    sb_ = K.banks[idx % 3]
                    it_, ip = idx % NTB, idx % NPB
                    for j, (blk, kprev, kcur, vprev, dprev, vcur) in enumerate(blocks_of(idx)):
                        qap = qz[:, b, hh, blk * 128:(blk + 1) * 128]
                        K.op("pe", lambda e: e.matmul(sb_.t[:, j * 256:j * 256 + 128], lhsT=kprev, rhs=qap, start=True, stop=True),
                             reads=[ldr[b]], writes=[sb_.res])
                        K.op("pe", lambda e: e.matmul(sb_.t[:, j * 256 + 128:j * 256 + 256], lhsT=kcur, rhs=qap, start=True, stop=True),
                             reads=[ldr[b]], writes=[sb_.res])
                    bias2 = AP(bt, (b * 2 + hh) * 256, [[pstep(bt), 128], [0, 2], [1, 256]])
                    K.op("dve", lambda e: e.scalar_tensor_tensor(out=tmp[:, it_, :].rearrange("p (a c) -> p a c", a=2),
                                                                 in0=sb_.t[:, :].rearrange("p (a c) -> p a c", a=2), scalar=60.0, in1=bias2,
                                                                 op0=ALU.min, op1=ALU.add), reads=[sb_.res, ldr[b]], writes=[tmpr[it_]])
                    K.op("act", lambda e: e.activation(out=pt[:, ip, :], in_=tmp[:, it_, :], func=AF.Exp), reads=[tmpr[it_]], writes=[ptr[ip]])

                def emit_pv(idx):
                    ci, qgp, hh, half = jobs[idx]
                    p, g = combos[ci]
                    D = DILS[g]
                    b = ci % 2
                    ip = idx % NPB
                    rows = slice(hh * 64, (hh + 1) * 64)
                    numb = K.banks[3 + 2 * (qgp % 2)]
                    denb = K.banks[4 + 2 * (qgp % 2)]
                    for j, (blk, kprev, kcur, vprev, dprev, vcur) in enumerate(blocks_of(idx)):
                        cs = slice((blk % 4) * 128, (blk % 4) * 128 + 128)
                        pprev = pt[:, ip, j * 256:j * 256 + 128]
                        pcur = pt[:, ip, j * 256 + 128:j * 256 + 256]
                        st = (hh == 0 and half == 0 and j == 0)
                        fin = (hh == 1 and half == 1 and j == 1)
                        K.op("pe", lambda e: e.matmul(numb.t[:, cs], lhsT=vprev, rhs=pprev, start=st, stop=False, skip_group_check=True), reads=[vzr[b], ptr[ip]], writes=[numb.res])
                        K.op("pe", lambda e: e.matmul(numb.t[:, cs], lhsT=vcur, rhs=pcur, start=False, stop=fin, skip_group_check=True), reads=[vzr[b], ptr[ip]], writes=[numb.res])
                        K.op("pe", lambda e: e.matmul(denb.t[:, cs], lhsT=dprev, rhs=pprev, start=st, stop=False, skip_group_check=True),
                             reads=[ozr, ptr[ip]], writes=[denb.res])
                        K.op("pe", lambda e: e.matmul(denb.t[:, cs], lhsT=oz[:, hh, :], rhs=pcur, start=False, stop=fin, skip_group_check=True),
                             reads=[ozr, ptr[ip]], writes=[denb.res])
                    if not (hh == 1 and half == 1):
                        return
                    if D == 1:
                        an, ad = accn[:, qgp * 512:(qgp + 1) * 512], accd[:, qgp * 512:(qgp + 1) * 512]
                        sn, sd = numb.t[:, :], denb.t[:, :]
                    elif D == 4:
                        an = AP(accn, qgp, [[NT, 128], [4, 512]])
                        ad = AP(accd, qgp, [[NT, 128], [4, 512]])
                        sn, sd = numb.t[:, :], denb.t[:, :]
                    else:
                        an = AP(accn, 4 * qgp, [[NT, 128], [1, 4], [16, 128]])
                        ad = AP(accd, 4 * qgp, [[NT, 128], [1, 4], [16, 128]])
                        sn = numb.t[:, :].rearrange("p (a c) -> p a c", a=4)
                        sd = denb.t[:, :].rearrange("p (a c) -> p a c", a=4)
                    if g == 0:
                        K.op("act", lambda e: e.copy(an, sn), reads=[numb.res], writes=[accr])
                        K.op("act", lambda e: e.copy(ad, sd), reads=[denb.res], writes=[accr])
                    else:
                        K.op("dve", lambda e: e.tensor_tensor(out=an, in0=sn, in1=an, op=ALU.add), reads=[numb.res, accr], writes=[accr])
                        K.op("dve", lambda e: e.tensor_tensor(out=ad, in0=sd, in1=ad, op=ALU.add), reads=[denb.res, accr], writes=[accr])
                    if g == 2 and qgp == 3:
                        K.op("act", lambda e: e.activation(out=accd[:], in_=accd[:], func=AF.Ln), reads=[accr], writes=[accr])
                        K.op("act", lambda e: e.activation(out=accd[:], in_=accd[:], func=AF.Exp, scale=-1.0), reads=[accr], writes=[accr])
                        K.op("pool", lambda e: e.tensor_tensor(out=oT[:, p, :], in0=accn[:], in1=accd[:], op=ALU.mult), reads=[accr], writes=otr[p])

                nj = len(jobs)
                load_pg(0)
                load_pg(1)
                for idx in range(nj + LA):
                    if idx < nj:
                        emit_score(idx)
                    if idx - LA >= 0:
                        emit_pv(idx - LA)
                        jc = jobs[idx - LA]
                        if jc[1] == 3 and jc[2] == 1 and jc[3] == 1 and jc[0] + 2 < len(combos):
                            load_pg(jc[0] + 2)
            with K.phase() as p3:
                self.ws.begin(p3)
                self.proj_F(lambda k, T: (oT[:, k, T * 512:(T + 1) * 512], otr[k][T]), 8, 8, self.add_to_h)

    def ssd_layer(self, l):
        K = self.K
        sl4 = lambda T: slice(T * 512, (T + 1) * 512)
        with K.phase() as ph:
            dtT = ph.sb("dtT", [128, 16, 32], F32)
            aT = ph.sb("aT", [128, 16, 32], F32)
            dar = Res()
            rawh = ph.sb("rawh", [128, 32, 8], F32)
            rawr = Res()
            fx = ph.sb("fx", [128, 32, 3], F32)
            fxr = Res()
            self.fx, self.fxr = fx, fxr
            with K.phase() as p1:
                self.ws.begin(p1)
                hn = p1.sb("hn", [128, 8, NT], BF16)
                hnr = mkres(4)
                self.rmsnorm(p1, 2 * l, lambda c, T: hn[:, c, sl4(T)], hnr)
                src = lambda k, T: (hn[:, k, sl4(T)], hnr[T])
                zst = p1.sb("zst", [128, 8, 512], F32)
                zr = Res()

                def evac_z(un, tt, bank):
                    K.op("act", lambda e: e.activation(out=zst[:, tt % 8, :], in_=bank.t[:, :], func=AF.Silu), reads=[bank.res], writes=[zr])
                    if tt % 8 == 7:
                        K.dma(AP(self.zs, (tt - 7) * 128 * 2048 + un * 512, [[2048, 128], [128 * 2048, 8], [1, 512]]), zst[:], reads=[zr], writes=[self.dr("zs")])

                self.proj_T(lambda k, tt: (hn[:, k, tt * 128:(tt + 1) * 128], hnr[tt // 4]), 4, evac_z)
                xb = p1.sb("xb", [128, 2, NT + 4], F32)
                xr = mkres(2)
                cacc = p1.sb("cacc", [128, 2, NT], F32)
                caccr = mkres(2)
                tl = p1.sb("tl", [128, 32, 4], F32)
                tlr = Res()
                K.op("dve", lambda e: e.memset(tl[:], 0.0), writes=[tlr])
                K.op("dve", lambda e: e.memset(xb[:, :, 0:3], 0.0), writes=xr)

                def evac_x(oc, T, bank):
                    i = oc % 2
                    K.op("act", lambda e: e.copy(xb[:, i, 3 + T * 512:3 + (T + 1) * 512], bank.t[:, :]), reads=[bank.res], writes=[xr[i]])
                    if T == 3:
                        K.op("dve", lambda e: e.tensor_copy(tl[:, oc, 0:3], xb[:, i, NT:NT + 3]), reads=[xr[i]], writes=[tlr])
                        K.op("dve", lambda e: e.tensor_copy(rawh[:, oc, 3:6], xb[:, i, 3:6]), reads=[xr[i]], writes=[rawr])
                        wj = lambda j: self.c[:, C_CONVW + oc * 4 + j:C_CONVW + oc * 4 + j + 1]
                        K.op("dve", lambda e: e.tensor_scalar(out=cacc[:, i, :], in0=xb[:, i, 0:NT], scalar1=wj(0), scalar2=self.c[:, C_CONVB + oc:C_CONVB + oc + 1],
                                                              op0=ALU.mult, op1=ALU.add), reads=[xr[i], self.cres], writes=[caccr[i]])
                        for j in range(1, 4):
                            K.op("dve", lambda e: e.scalar_tensor_tensor(out=cacc[:, i, :], in0=xb[:, i, j:j + NT], scalar=wj(j), in1=cacc[:, i, :], op0=ALU.mult, op1=ALU.add),
                                 reads=[xr[i], caccr[i], self.cres], writes=[caccr[i]])
                        K.op("act", lambda e: e.activation(out=cacc[:, i, :], in_=cacc[:, i, :], func=AF.Silu), reads=[caccr[i]], writes=[caccr[i]])
                        K.dma(AP(self.xbc_act, oc * 128 * 2048, [[2048, 128], [1, 2048]]), cacc[:, i, :], reads=[caccr[i]], writes=[self.dr("xbc_act")])

                self.proj_F(src, 32, 8, evac_x)
                K.dma(self.tail_in.ap(), tl[:].rearrange("p a b -> p (a b)"), reads=[tlr], writes=[self.dr("tail_in")])
                K.coll(self.tail_in, self.tail_out, reads=[self.dr("tail_in")], writes=[self.dr("tail_out")])
                dtf = cacc[0:32, 0, :]
                af = cacc[0:32, 1, :]
                dfr = Res()
                K.op("dve", lambda e: e.tensor_copy(self.misc[:, 2:3], self.misc[:, 3:4]), reads=[self.miscres], writes=[dfr, self.miscres] + caccr)
                Ac = p1.sb("Ac", [32, 2], F32)
                Ar = Res()
                K.op("act", lambda e: e.activation(out=Ac[:, 0:1], in_=self.c[0:32, C_ALOG:C_ALOG + 1], func=AF.Exp), reads=[self.cres], writes=[Ar])
                K.op("dve", lambda e: e.tensor_scalar(out=Ac[:, 1:2], in0=Ac[:, 0:1], scalar1=-1.0, scalar2=None, op0=ALU.mult), reads=[Ar], writes=[Ar])

                def evac_dt(oc, T, bank):
                    K.op("act", lambda e: e.activation(out=dtf[:, sl4(T)], in_=bank.t[0:32, :], func=AF.Exp, bias=self.c[0:32, C_DTB:C_DTB + 1], scale=1.0),
                         reads=[bank.res, self.cres], writes=[dfr])
                    K.op("act", lambda e: e.activation(out=dtf[:, sl4(T)], in_=dtf[:, sl4(T)], func=AF.Ln, bias=1.0, scale=1.0), reads=[dfr], writes=[dfr])
                    K.op("dve", lambda e: e.tensor_scalar(out=af[:, sl4(T)], in0=dtf[:, sl4(T)], scalar1=Ac[:, 1:2], scalar2=None, op0=ALU.mult),
                         reads=[dfr, Ar], writes=[dfr])

                self.proj_F(src, 1, 8, evac_dt)
                for c in range(16):
                    bank = K.banks[4 + c % 2]
                    K.op("pe", lambda e: e.transpose(out=bank.t[:, 0:32], in_=dtf[:, c * 128:(c + 1) * 128], identity=self.c[0:32, C_IDENT:C_IDENT + 32]),
                         reads=[dfr, self.cres], writes=[bank.res])
                    K.op("pe", lambda e: e.transpose(out=bank.t[:, 32:64], in_=af[:, c * 128:(c + 1) * 128], identity=self.c[0:32, C_IDENT:C_IDENT + 32]),
                         reads=[dfr, self.cres], writes=[bank.res])
                    K.op("act", lambda e: e.copy(dtT[:, c, :], bank.t[:, 0:32]), reads=[bank.res], writes=[dar])
                    K.op("act", lambda e: e.copy(aT[:, c, :], bank.t[:, 32:64]), reads=[bank.res], writes=[dar])
            with K.phase() as p2:
                K.dma(rawh[:, :, 0:3], AP(self.tail_out, 0, [[128, 128], [4, 32], [1, 3]]), reads=[self.dr("tail_out")], writes=[rawr])
                K.op("dve", lambda e: e.tensor_scalar(out=rawh[:, :, 0:3], in0=rawh[:, :, 0:3], scalar1=self.flag(), scalar2=None, op0=ALU.mult),
                     reads=[rawr, self.cres], writes=[rawr])
                tf = p2.sb("tf", [128, 32, 3], F32)
                wv = lambda j: AP(self.c, C_CONVW + j, [[CST_W, 128], [4, 32], [0, 3]])
                K.op("dve", lambda e: e.tensor_tensor(out=fx[:], in0=rawh[:, :, 0:3], in1=wv(0), op=ALU.mult), reads=[rawr, self.cres], writes=[fxr])
                K.op("dve", lambda e: e.tensor_tensor(out=fx[:], in0=fx[:], in1=AP(self.c, C_CONVB, [[CST_W, 128], [1, 32], [0, 3]]), op=ALU.add),
                     reads=[fxr, self.cres], writes=[fxr])
                for j in range(1, 4):
                    K.op("dve", lambda e: e.tensor_tensor(out=tf[:], in0=rawh[:, :, j:j + 3], in1=wv(j), op=ALU.mult), reads=[rawr, self.cres, fxr], writes=[fxr])
                    K.op("dve", lambda e: e.tensor_tensor(out=fx[:], in0=fx[:], in1=tf[:], op=ALU.add), reads=[fxr], writes=[fxr])
                K.op("act", lambda e: e.activation(out=fx[:], in_=fx[:], func=AF.Silu), reads=[fxr], writes=[fxr])
            for states_only in (True, False):
                with K.phase() as p3:
                    self.ssd_scan(p3, dtT, aT, dar, states_only)
            with K.phase() as p4:
                self.ws.begin(p4)
                ysb = p4.sb("ysb", [128, 16, NT], BF16)
                ysr = mkres(4)
                for T in range(4):
                    K.dma(ysb[:, :, sl4(T)], AP(self.yT_d, T * 512, [[2048, 128], [128 * 2048, 16], [1, 512]]), reads=[self.dr("yT_d")], writes=[ysr[T]])
                self.proj_F(lambda k, T: (ysb[:, k, sl4(T)], ysr[T]), 8, 16, self.add_to_h)

    def ssd_scan(self, ph, dtT, aT, dar, states_only):
        K = self.K
        sb = ph.sb
        stT = sb("stT", [128, 2048], F32)
        strg = mkres(8)
        prevb = sb("prevb", [128, 2048], BF16)
        pvrg = mkres(8)
        if states_only:
            K.op("dve", lambda e: e.memset(stT[:], 0.0), writes=strg)
        else:
            K.dma(stT[:], AP(self.st_out, 0, [[2048, 128], [1, 2048]]), reads=[self.dr("st_out")], writes=strg)
            K.op("dve", lambda e: e.tensor_scalar(out=stT[:], in0=stT[:], scalar1=self.flag(), scalar2=None, op0=ALU.mult), reads=strg + [self.cres], writes=strg)
            K.op("pool", lambda e: e.tensor_copy(prevb[:], stT[:]), reads=strg, writes=pvrg)
        nfc = 24 if states_only else 32
        xa2 = sb("xa", [128, 2, 32, 128], F32)
        xar2 = mkres(2)
        x_tm = sb("x_tm", [128, 2048], F32)
        xtr = Res()
        xdt = sb("xdt", [128, 2048], BF16)
        xw = sb("xw", [128, 2048], BF16)
        xdr, xwr = Res(), Res()
        Btm = sb("Btm", [128, 8, 128], BF16)
        btr = Res()
        sm = sb("sm", [128, 6, 32], F32)
        smr = Res()
        if not states_only:
            zc2 = sb("zc", [128, 2, 2048], F32)
            zcr2 = mkres(2)
            bcb = sb("bcb", [128, 16, 128], BF16)
            bcr = Res()
            cbm2 = sb("cbm", [128, 2, 128], F32)
            cbr2 = mkres(2)
            La2 = sb("La", [128, 2, 4, 128], F32)
            lar2 = mkres(2)
            dec2 = sb("dec", [128, 2, 4, 128], F32)
            der2 = mkres(2)
            MT2 = sb("MT", [128, 2, 4, 128], BF16)
            mtr2 = mkres(2)
            yo = sb("yo", [128, 512], F32)
            yor = Res()
            cbres = mkres(2)
            y = sb("y", [128, 2048], F32)
            yr = Res()
            junk = sb("junk", [128, 256], F32)
            ss = sb("ss", [128, 16], F32)
            ssr = Res()
            gg = sb("gg", [128, 2048], F32)
            Dh = sb("Dh", [128, 32], F32)
            ggr = Res()
            K.dma(gg[:], AP(self.rep, R_GG, [[REP_W, 128], [1, 2048]]), writes=[ggr])
            K.dma(Dh[:], AP(self.rep, R_DH, [[REP_W, 128], [1, 32]]), writes=[ggr])
            yTc = sb("yTc", [128, 16, 128], BF16)
            ytr = Res()
        ones, triU, maskL, ident = self.ones(), self.triU(), self.maskL(), self.ident()
        def load_chunk(c):
            K.dma(xa2[:, c % 2, 0:nfc, :], AP(self.xbc_act, c * 128, [[2048, 128], [128 * 2048, nfc], [1, 128]]), reads=[self.dr("xbc_act")], writes=[xar2[c % 2]])
            if not states_only:
                K.dma(zc2[:, c % 2, :], AP(self.zs, c * 128 * 2048, [[2048, 128], [1, 2048]]), reads=[self.dr("zs")], writes=[zcr2[c % 2]])

        load_chunk(0)
        K.op("dve", lambda e: e.tensor_copy(xa2[:, 0, 0:nfc, 0:3], self.fx[:, 0:nfc, :]), reads=[self.fxr, xar2[0]], writes=[xar2[0]])
        for c in range(16):
            if c + 1 < 16:
                load_chunk(c + 1)
            xa, xar = xa2[:, c % 2, :, :], xar2[c % 2]
            if not states_only:
                zc, zcr = zc2[:, c % 2, :], zcr2[c % 2]
            b6 = K.banks[6]
            K.op("pe", lambda e: e.matmul(b6.t[:, 0:32], lhsT=triU, rhs=aT[:, c, :], start=True, stop=True), reads=[dar, self.cres], writes=[b6.res])
            K.op("pe", lambda e: e.matmul(b6.t[:, 32:64], lhsT=ones, rhs=aT[:, c, :], start=True, stop=True), reads=[dar, self.cres], writes=[b6.res])
            K.op("act", lambda e: e.copy(sm[:, 0:2, :].rearrange("p a b -> p (a b)"), b6.t[:, 0:64]), reads=[b6.res], writes=[smr])
            K.op("dve", lambda e: e.tensor_tensor(out=sm[:, 2, :], in0=sm[:, 1, :], in1=sm[:, 0, :], op=ALU.subtract), reads=[smr], writes=[smr])
            K.op("act", lambda e: e.activation(out=sm[:, 2, :], in_=sm[:, 2, :], func=AF.Exp), reads=[smr], writes=[smr])
            K.op("act", lambda e: e.activation(out=sm[:, 3, :], in_=sm[:, 1, :], func=AF.Exp), reads=[smr], writes=[smr])
            K.op("act", lambda e: e.activation(out=sm[:, 5, :], in_=sm[:, 0, :], func=AF.Exp), reads=[smr], writes=[smr])
            K.op("dve", lambda e: e.tensor_tensor(out=sm[:, 4, :], in0=sm[:, 2, :], in1=dtT[:, c, :], op=ALU.mult), reads=[smr, dar], writes=[smr])
            for q in range(4):
                bk = K.banks[q % 2]
                for j in range(4):
                    K.op("pe", lambda e: e.transpose(out=bk.t[:, j * 128:(j + 1) * 128], in_=xa[:, 4 * q + j, :], identity=ident), reads=[xar, self.cres], writes=[bk.res])
                K.op("act", lambda e: e.copy(x_tm[:, q * 512:(q + 1) * 512], bk.t[:, :]), reads=[bk.res], writes=[xtr])
            x3 = x_tm[:].rearrange("p (h d) -> p h d", d=64)
            K.op("pool", lambda e: e.tensor_tensor(out=xw[:].rearrange("p (h d) -> p h d", d=64), in0=x3, in1=AP(sm, 4 * 32, [[192, 128], [1, 32], [0, 64]]), op=ALU.mult),
                 reads=[xtr, smr], writes=[xwr])
            if not states_only:
                K.op("pool", lambda e: e.tensor_tensor(out=xdt[:].rearrange("p (h d) -> p h d", d=64), in0=x3, in1=AP(dtT, c * 32, [[512, 128], [1, 32], [0, 64]]), op=ALU.mult),
                     reads=[xtr, dar], writes=[xdr])
                K.op("act", lambda e: e.copy(bcb[:], xa[:, 16:32, :]), reads=[xar], writes=[bcr])
            for q in range(2):
                bk = K.banks[2 + q]
                for j in range(4):
                    K.op("pe", lambda e: e.transpose(out=bk.t[:, j * 128:(j + 1) * 128], in_=xa[:, 16 + 4 * q + j, :], identity=ident), reads=[xar, self.cres], writes=[bk.res])
                K.op("act", lambda e: e.copy(Btm[:, 4 * q:4 * q + 4, :].rearrange("p a b -> p (a b)"), bk.t[:, :]), reads=[bk.res], writes=[btr])
            def front(g):
                pg = g % 2
                cbm, cbr = cbm2[:, pg, :], cbr2[pg]
                La, lar = La2[:, pg, :, :], lar2[pg]
                dec, der = dec2[:, pg, :, :], der2[pg]
                MT, mtr = MT2[:, pg, :, :], mtr2[pg]
                cbc = slice(128 + pg * 128, 256 + pg * 128)
                K.op("pe", lambda e: e.matmul(b6.t[:, cbc], lhsT=bcb[:, g, :], rhs=bcb[:, 8 + g, :], start=True, stop=True), reads=[bcr], writes=[cbres[pg]])
                K.op("dve", lambda e: e.tensor_tensor(out=cbm, in0=b6.t[:, cbc], in1=triU, op=ALU.mult), reads=[cbres[pg], self.cres], writes=[cbr])
                abc = AP(aT, c * 32 + 4 * g, [[512, 128], [1, 4], [0, 128]])
                K.op("pool", lambda e: e.tensor_tensor(out=La, in0=AP(self.c, C_MASKL, [[CST_W, 128], [0, 4], [1, 128]]), in1=abc, op=ALU.mult),
                     reads=[dar, self.cres], writes=[lar])
                b0 = K.banks[0 + pg]
                for j in range(4):
                    K.op("pe", lambda e: e.matmul(b0.t[:, j * 128:(j + 1) * 128], lhsT=La[:, j, :], rhs=triU, start=True, stop=True), reads=[lar, self.cres], writes=[b0.res])
                K.op("act", lambda e: e.activation(out=dec.rearrange("p a b -> p (a b)"), in_=b0.t[:, :], func=AF.Exp), reads=[b0.res], writes=[der])
                K.op("dve", lambda e: e.tensor_tensor(out=MT, in0=dec, in1=AP(cbm2, pg * 128, [[256, 128], [0, 4], [1, 128]]), op=ALU.mult), reads=[der, cbr], writes=[mtr])

            def back(g):
                gsl = slice(g * 256, (g + 1) * 256)
                pg = g % 2
                yb = K.banks[4 + (g // 2) % 2]
                yob = K.banks[2 + (g // 2) % 2]
                if not states_only:
                    MT, mtr = MT2[:, pg, :, :], mtr2[pg]
                    for j in range(4):
                        h = 4 * g + j
                        col = (g % 2) * 256 + j * 64
                        K.op("pe", lambda e: e.matmul(yb.t[:, col:col + 64], lhsT=MT[:, j, :], rhs=xdt[:, h * 64:(h + 1) * 64], start=True, stop=True),
                             reads=[mtr, xdr], writes=[yb.res])
                    K.op("pe", lambda e: e.matmul(yob.t[:, (g % 2) * 256:(g % 2) * 256 + 256], lhsT=bcb[:, 8 + g, :], rhs=prevb[:, gsl], start=True, stop=True),
                         reads=[bcr, pvrg[g]], writes=[yob.res])
                csb = K.banks[7]
                K.op("pe", lambda e: e.matmul(csb.t[:, 0:256], lhsT=Btm[:, g, :], rhs=xw[:, gsl], start=True, stop=True), reads=[btr, xwr], writes=[csb.res])
                K.op("dve", lambda e: e.tensor_tensor(out=stT[:, gsl].rearrange("p (h d) -> p h d", d=64), in0=stT[:, gsl].rearrange("p (h d) -> p h d", d=64),
                                                      in1=AP(sm, 3 * 32 + 4 * g, [[192, 128], [1, 4], [0, 64]]), op=ALU.mult), reads=[strg[g], smr], writes=[strg[g]])
                K.op("dve", lambda e: e.tensor_tensor(out=stT[:, gsl], in0=csb.t[:, 0:256], in1=stT[:, gsl], op=ALU.add), reads=[csb.res, strg[g]], writes=[strg[g]])
                if not states_only:
                    K.op("act", lambda e: e.copy(prevb[:, gsl], stT[:, gsl]), reads=[strg[g]], writes=[pvrg[g]])
                    if g % 2 == 1:
                        q = g // 2
                        bsl = slice(q * 512, (q + 1) * 512)
                        y3 = y[:, bsl].rearrange("p (h d) -> p h d", d=64)
                        K.op("dve", lambda e: e.tensor_tensor(out=y3, in0=x_tm[:, bsl].rearrange("p (h d) -> p h d", d=64),
                                                              in1=AP(Dh, 8 * q, [[32, 128], [1, 8], [0, 64]]), op=ALU.mult), reads=[xtr, ggr], writes=[yr])
                        K.op("dve", lambda e: e.tensor_tensor(out=y[:, bsl], in0=yb.t[:, :], in1=y[:, bsl], op=ALU.add), reads=[yb.res, yr], writes=[yr])
                        K.op("dve", lambda e: e.tensor_tensor(out=yo[:].rearrange("p (h d) -> p h d", d=64), in0=yob.t[:, :].rearrange("p (h d) -> p h d", d=64),
                                                              in1=AP(sm, 5 * 32 + 8 * q, [[192, 128], [1, 8], [0, 64]]), op=ALU.mult), reads=[yob.res, smr], writes=[yor])
                        K.op("dve", lambda e: e.tensor_tensor(out=y[:, bsl], in0=y[:, bsl], in1=yo[:], op=ALU.add), reads=[yr, yor], writes=[yr])
                        K.op("dve", lambda e: e.tensor_tensor(out=y[:, bsl], in0=y[:, bsl], in1=zc[:, bsl], op=ALU.mult), reads=[yr, zcr], writes=[yr])

            if not states_only:
                front(0)
            for g in range(8):
                if not states_only and g + 1 < 8:
                    front(g + 1)
                back(g)
            if states_only:
                continue
            for g8 in range(8):
                K.op("act", lambda e: e.activation(out=junk[:], in_=y[:, g8 * 256:(g8 + 1) * 256], func=AF.Square, accum_out=ss[:, g8:g8 + 1]), reads=[yr], writes=[ssr])
            K.op("act", lambda e: e.activation(out=ss[:, 8:16], in_=ss[:, 0:8], func=AF.Ln, scale=1.0 / 256, bias=EPS), reads=[ssr], writes=[ssr])
            K.op("act", lambda e: e.activation(out=ss[:, 8:16], in_=ss[:, 8:16], func=AF.Exp, scale=-0.5), reads=[ssr], writes=[ssr])
            for g8 in range(8):
                gs = slice(g8 * 256, (g8 + 1) * 256)
                K.op("dve", lambda e: e.scalar_tensor_tensor(out=y[:, gs], in0=y[:, gs], scalar=ss[:, 8 + g8:9 + g8], in1=gg[:, gs], op0=ALU.mult, op1=ALU.mult),
                     reads=[yr, ssr, ggr], writes=[yr])
            for q in range(4):
                bk = K.banks[2 + q % 2]
                for j in range(4):
                    fcx = 4 * q + j
                    K.op("pe", lambda e: e.transpose(out=bk.t[:, j * 128:(j + 1) * 128], in_=y[:, fcx * 128:(fcx + 1) * 128], identity=ident), reads=[yr, self.cres], writes=[bk.res])
                K.op("act", lambda e: e.copy(yTc[:, 4 * q:4 * q + 4, :].rearrange("p a b -> p (a b)"), bk.t[:, :]), reads=[bk.res], writes=[ytr])
            K.dma(AP(self.yT_d, c * 128, [[2048, 128], [128 * 2048, 16], [1, 128]]), yTc[:], reads=[ytr], writes=[self.dr("yT_d")])
        if states_only:
            K.dma(self.st_in.ap(), stT[:], reads=strg, writes=[self.dr("st_in")])
            K.coll(self.st_in, self.st_out, reads=[self.dr("st_in")], writes=[self.dr("st_out")])


C_ONES, C_IDENT, C_TRIU, C_MASKL = 0, 128, 256, 384
C_GAIN = 512
C_FLAG = C_GAIN + 72
C_SUBLN = C_FLAG + 1
C_DTB = C_SUBLN + 1
C_ALOG = C_DTB + 1
C_CONVW = C_ALOG + 1
C_CONVB = C_CONVW + 128
CST_W = C_CONVB + 32
R_LAM = 0
R_D = 256
R_GG = R_D + 2048
R_AREP = R_GG + 2048
R_DH = R_AREP + 32
REP_W = R_DH + 32
OH_W = 1536 + 4608


def rel_bucket_np(dist):
    dist = np.asarray(dist, dtype=np.int64)
    d = np.maximum(dist, 1).astype(np.float32)
    large = 16 + (np.log(d / np.float32(16)) / np.float32(math.log(2048 / 16)) * np.float32(16)).astype(np.int32)
    large = np.minimum(large, 31)
    return np.where(dist < 16, dist, large)


def build_onehot():
    oh = np.zeros((33, OH_W), np.float32)
    for g, D in enumerate(DILS):
        for which in range(2):
            for x in range(255):
                if which == 1:
                    steps = x - 127
                    valid = steps >= 0
                else:
                    steps = x + 1
                    valid = steps <= 128
                col = (g * 2 + which) * 256 + x
                if valid:
                    oh[int(rel_bucket_np(steps * D)), col] = 1.0
                else:
                    oh[32, col] = 1.0
            oh[32, (g * 2 + which) * 256 + 255] = 1.0
    xs = np.arange(4608)
    dist = xs - 511
    bk = rel_bucket_np(np.maximum(dist, 0))
    for x in range(4608):
        if dist[x] >= 0:
            oh[int(bk[x]), 1536 + x] = 1.0
        else:
            oh[32, 1536 + x] = 1.0
    return oh


def pack_F(W, KC):
    W = np.asarray(W, np.float32)
    n = W.shape[1] // 128
    return [W[:, o * 128:(o + 1) * 128].reshape(KC, 128, 128).transpose(1, 0, 2).reshape(128, KC * 128) for o in range(n)]


def units_F(W, KC):
    blocks = pack_F(W, KC)
    per = 2048 // (KC * 128)
    out = []
    for i in range(0, len(blocks), per):
        bl = blocks[i:i + per]
        while len(bl) < per:
            bl.append(np.zeros_like(bl[0]))
        out.append(np.concatenate(bl, axis=1))
    return out


def units_T(W):
    W = np.asarray(W, np.float32)
    n = W.shape[1] // 512
    out = []
    for c in range(n):
        blk = W[:, c * 512:(c + 1) * 512].reshape(8, 128, 512).transpose(1, 0, 2)
        out.append(np.ascontiguousarray(blk[:, 0:4, :]).reshape(128, 2048))
        out.append(np.ascontiguousarray(blk[:, 4:8, :]).reshape(128, 2048))
    return out


def build_stream(inp, layers, skip_mixer=False, skip_mlp=False):
    units = []
    for l in layers:
        pre = "l%d_" % l
        kind = l % 3
        if skip_mixer:
            pass
        elif kind == 0:
            W = inp[pre + "dil_w_qkv"].reshape(1024, 3, 3, 1024)
            for g in range(3):
                units += units_F(W[:, g, 1], 8)
                units += units_T(W[:, g, 2])
                units += units_F(W[:, g, 0], 8)
            units += units_F(inp[pre + "dil_w_o"], 8)
        elif kind == 1:
            W = inp[pre + "diff_w_qkv"]
            units += units_T(W[:, 2048:3072])
            units += units_F(W[:, 1024:2048], 8)
            units += units_F(W[:, 0:1024], 8)
            units += units_F(inp[pre + "diff_w_o"], 8)
        else:
            W = inp[pre + "ssm_w_in"]
            units += units_T(W[:, 0:2048])
            units += units_F(W[:, 2048:6144], 8)
            wdt = np.zeros((1024, 128), np.float32)
            wdt[:, 0:32] = W[:, 6144:6176]
            units += units_F(wdt, 8)
            units += units_F(inp[pre + "ssm_w_out"], 16)
        up, dn = inp[pre + "mlp_w_up"], inp[pre + "mlp_w_down"]
        for G in range(0 if skip_mlp else 4):
            units += units_F(up[:, G * 1024:(G + 1) * 1024], 8)
            units += units_F(dn[G * 1024:(G + 1) * 1024, :], 8)
    return np.ascontiguousarray(np.stack(units, axis=0)) if units else np.zeros((0, 128, 2048), np.float32)


def build_consts(inp, half):
    c = np.zeros((128, CST_W), np.float32)
    c[:, C_ONES:C_ONES + 128] = 1.0
    c[:, C_IDENT:C_IDENT + 128] = np.eye(128, dtype=np.float32)
    s = np.arange(128)[:, None]
    t = np.arange(128)[None, :]
    c[:, C_TRIU:C_TRIU + 128] = (s <= t)
    c[:, C_MASKL:C_MASKL + 128] = (s > t)
    names = ["l0_mix_norm", "l0_mlp_norm", "l1_mix_norm", "l1_mlp_norm", "l2_mix_norm", "l2_mlp_norm", "l3_mix_norm", "l3_mlp_norm", "final_norm"]
    for i, nm in enumerate(names):
        c[:, C_GAIN + i * 8:C_GAIN + i * 8 + 8] = np.asarray(inp[nm], np.float32).reshape(8, 128).T
    c[:, C_FLAG] = float(half)
    c[:, C_SUBLN] = np.asarray(inp["l1_diff_subln"], np.float32)
    c[0:32, C_DTB] = np.asarray(inp["l2_ssm_dt_bias"], np.float32)
    c[0:32, C_ALOG] = np.asarray(inp["l2_ssm_A_log"], np.float32)
    cw = np.asarray(inp["l2_ssm_conv_w"], np.float32)
    c[:, C_CONVW:C_CONVW + 128] = cw.reshape(4, 32, 128).transpose(2, 1, 0).reshape(128, 128)
    c[:, C_CONVB:C_CONVB + 32] = np.asarray(inp["l2_ssm_conv_b"], np.float32).reshape(32, 128).T
    r = np.zeros((128, REP_W), np.float32)
    lam = np.concatenate([np.asarray(inp["l1_diff_lam_" + k], np.float32) for k in ("q1", "k1", "q2", "k2")])
    r[:, R_LAM:R_LAM + 256] = lam[None, :]
    r[:, R_D:R_D + 2048] = np.repeat(np.asarray(inp["l2_ssm_D"], np.float32), 64)[None, :]
    r[:, R_GG:R_GG + 2048] = np.asarray(inp["l2_ssm_gate_norm"], np.float32)[None, :]
    r[:, R_AREP:R_AREP + 32] = np.asarray(inp["l2_ssm_A_log"], np.float32)[None, :]
    r[:, R_DH:R_DH + 32] = np.asarray(inp["l2_ssm_D"], np.float32)[None, :]
    return c, r


LAYERS = (0, 1, 2, 3)
_CACHE = {}


def run(inp, layers=LAYERS, skip_mixer=False, skip_mlp=False, stop=None):
    inp = {k: np.asarray(v) for k, v in inp.items()}
    wstream = build_stream(inp, layers, skip_mixer, skip_mlp)
    if wstream.shape[0] == 0:
        wstream = np.zeros((1, 128, 2048), np.float32)
    oh = build_onehot()
    key = (tuple(layers), wstream.shape[0], skip_mixer, skip_mlp, stop)
    if key not in _CACHE:
        _CACHE[key] = Builder(list(layers), wstream.shape[0], skip_mixer, skip_mlp, stop).build()
    nc = _CACHE[key]
    x = np.asarray(inp["x"], np.float32)
    relb = np.ascontiguousarray(np.asarray(inp["rel_bias"], np.float32))
    in_maps = []
    for core in range(8):
        b, half = core // 2, core % 2
        c, r = build_consts(inp, half)
        xT = np.ascontiguousarray(x[b, half * NT:(half + 1) * NT, :].T)
        in_maps.append({"xT": xT, "wstream": wstream, "cst": c, "rep": r, "oh": oh, "relb": relb})
    res = run_bass_kernel_spmd(nc, in_maps, core_ids=list(range(8)))
    out = np.zeros((4, 4096, 1024), np.float32)
    for core in range(8):
        b, half = core // 2, core % 2
        out[b, half * NT:(half + 1) * NT, :] = np.asarray(res.results[core]["outT"], np.float32).T
    return out


INPUT_NAMES = (
    "x", "rel_bias",
    "l0_mix_norm", "l0_dil_w_qkv", "l0_dil_w_o", "l0_mlp_norm", "l0_mlp_w_up", "l0_mlp_w_down",
    "l1_mix_norm", "l1_diff_w_qkv", "l1_diff_lam_q1", "l1_diff_lam_k1", "l1_diff_lam_q2", "l1_diff_lam_k2",
    "l1_diff_subln", "l1_diff_w_o", "l1_mlp_norm", "l1_mlp_w_up", "l1_mlp_w_down",
    "l2_mix_norm", "l2_ssm_w_in", "l2_ssm_conv_w", "l2_ssm_conv_b", "l2_ssm_dt_bias", "l2_ssm_A_log", "l2_ssm_D",
    "l2_ssm_gate_norm", "l2_ssm_w_out", "l2_mlp_norm", "l2_mlp_w_up", "l2_mlp_w_down",
    "l3_mix_norm", "l3_dil_w_qkv", "l3_dil_w_o", "l3_mlp_norm", "l3_mlp_w_up", "l3_mlp_w_down",
    "final_norm",
)


def kernel(**inputs):
    missing = [n for n in INPUT_NAMES if n not in inputs]
    assert not missing, missing
    return run(inputs, LAYERS)
```

```python
import math
import contextlib
import numpy as np
import concourse.bass as bass
import concourse.mybir as mybir
from concourse.bass_utils import run_bass_kernel_spmd

F32 = mybir.dt.float32
BF16 = mybir.dt.bfloat16
AF = mybir.ActivationFunctionType
ALU = mybir.AluOpType
AX = mybir.AxisListType

NT = 2048
EPS = 1e-5
EPOCH = 12000
NEG = -30000.0
PAIRS = [[0, 1], [2, 3], [4, 5], [6, 7]]
DILS = (1, 4, 16)


def lambda_init(layer):
    return 0.8 - 0.6 * math.exp(-0.3 * layer)


class Res:
    __slots__ = ("w", "r")

    def __init__(self):
        self.w = None
        self.r = {}


def mkres(n):
    return [Res() for _ in range(n)]


class Eng:
    def __init__(self, K, name, e, is_pe=False):
        self.K, self.name, self.e, self.is_pe = K, name, e, is_pe
        self.sems = []
        self.semnums = set()
        self.cnt = 0
        self.seen = {}

    def tick(self):
        if not self.sems or self.cnt >= EPOCH:
            s = self.K.new_sem(self.name + str(len(self.sems)))
            self.sems.append(s)
            self.semnums.add(s.num)
            self.cnt = 0
        self.cnt += 1
        s = self.sems[-1]
        self.K.latest[s.num] = (s, self.cnt)
        return (s, self.cnt)

    def wait(self, sem, val):
        if self.seen.get(sem.num, 0) >= val:
            return
        self.e.wait_ge(sem, val)
        self.seen[sem.num] = val


class Bank:
    def __init__(self, t):
        self.t = t
        self.res = Res()


class Kern:
    def __init__(self, nc, es):
        self.nc, self.es = nc, es
        self.latest = {}
        self.nsem = 0
        self.E = {
            "pe": Eng(self, "pe", nc.tensor, True),
            "act": Eng(self, "act", nc.scalar),
            "dve": Eng(self, "dve", nc.vector),
            "pool": Eng(self, "pool", nc.gpsimd),
            "sp": Eng(self, "sp", nc.sync),
        }
        self.dsem = []
        self.dval = []
        self.di = 0
        self.NDMA = 24
        self.banks = []
        self.uid = 0
        self.pending = []

    def new_sem(self, name):
        self.nsem += 1
        return self.es.enter_context(self.nc.semaphore("s_" + name + "_" + str(self.nsem)))

    def _deps(self, eng, reads, writes):
        d = {}

        def add(t):
            s, v = t
            if d.get(s.num, (None, 0))[1] < v:
                d[s.num] = (s, v)

        for r in reads:
            if r.w is not None:
                add(r.w)
        for w in writes:
            if w.w is not None:
                add(w.w)
            for t in w.r.values():
                add(t)
        for s, v in d.values():
            if eng.is_pe and s.num in eng.semnums:
                continue
            eng.wait(s, v)

    def _mark(self, tk, reads, writes):
        for r in reads:
            r.r[tk[0].num] = tk
        for w in writes:
            w.w = tk
            w.r = {}

    def op(self, en, emit, reads=(), writes=()):
        eng = self.E[en]
        self._deps(eng, reads, writes)
        ins = emit(eng.e)
        tk = eng.tick()
        ins.then_inc(tk[0], 1)
        self._mark(tk, reads, writes)

    def dma(self, out, in_, reads=(), writes=(), q="sp"):
        eng = self.E[q]
        self._deps(eng, reads, writes)
        if len(self.dsem) < self.NDMA:
            self.dsem.append(self.new_sem("dma"))
            self.dval.append(0)
        i = self.di % len(self.dsem) if len(self.dsem) == self.NDMA else len(self.dsem) - 1
        self.di += 1
        s = self.dsem[i]
        if self.dval[i] > 0:
            eng.wait(s, self.dval[i])
        ins = eng.e.dma_start(out=out, in_=in_)
        self.dval[i] += 16
        ins.then_inc(s, 16)
        tk = (s, self.dval[i])
        self.latest[s.num] = tk
        self._mark(tk, reads, writes)

    def coll(self, in_t, out_t, reads, writes):
        eng = self.E["pool"]
        self._deps(eng, reads, writes)
        s = self.new_sem("cc")
        cc = eng.e.collective_compute("AllGather", ALU.bypass, ins=[in_t.ap().opt()], outs=[out_t.ap().opt()],
                                      replica_groups=PAIRS)
        cc.then_inc(s)
        tk = (s, 1)
        self.latest[s.num] = tk
        self._mark(tk, reads, writes)

    def barrier(self):
        for eng in self.E.values():
            for s, v in list(self.latest.values()):
                if eng.is_pe and s.num in eng.semnums:
                    continue
                eng.wait(s, v)

    @contextlib.contextmanager
    def phase(self):
        ph = Phase(self)
        with ph.es:
            yield ph
            self.barrier()


class Phase:
    def __init__(self, K):
        self.K = K
        self.es = contextlib.ExitStack()

    def sb(self, name, shape, dt):
        self.K.uid += 1
        return self.es.enter_context(self.K.nc.sbuf_tensor(name + str(self.K.uid), list(shape), dt))


def AP(t, offset, dims):
    return bass.AP(t, offset, [list(d) for d in dims])


def pstep(t):
    n = 1
    for s in list(t.shape)[1:]:
        n *= int(s)
    return n


class WStream:
    NS = 3

    def __init__(self, K, wd):
        self.K, self.wd = K, wd
        self.nu = int(wd.shape[0])
        self.pos = 0
        self.dpos = 0
        self.cpos = 0

    def begin(self, ph, cast="pool"):
        self.cast_eng = cast
        self.st = ph.sb("wst", [128, self.NS, 2048], F32)
        self.bf = ph.sb("wbf", [128, self.NS, 2048], BF16)
        self.str_ = mkres(self.NS)
        self.bfr = mkres(self.NS)
        self.dpos = self.pos
        self.cpos = self.pos
        self._fill()

    def _dma(self):
        if self.dpos >= self.nu:
            return
        i = self.dpos % self.NS
        self.K.dma(self.st[:, i, :], self.wd[self.dpos], reads=[], writes=[self.str_[i]])
        self.dpos += 1
        K = self.K
        K.pend_cnt = getattr(K, "pend_cnt", 0) + 1
        if K.pending and K.pend_cnt >= K.pending[0][0]:
            K.pend_cnt = 0
            K.pending.pop(0)[2]()

    def _cast(self):
        if self.cpos >= self.nu or self.cpos >= self.dpos:
            return
        i = self.cpos % self.NS
        self.K.op(self.cast_eng, lambda e: e.tensor_copy(self.bf[:, i, :], self.st[:, i, :]),
                  reads=[self.str_[i]], writes=[self.bfr[i]])
        self.cpos += 1

    def _fill(self):
        while self.dpos < self.pos + self.NS - 1:
            if self.dpos >= self.nu:
                break
            self._dma()
        while self.cpos < self.pos + 1 and self.cpos < self.dpos:
            self._cast()

    def get(self):
        u = self.pos
        while self.cpos <= u:
            if self.dpos <= self.cpos:
                self._dma()
            self._cast()
        i = u % self.NS
        self.pos += 1
        self._fill()
        return self.bf[:, i, :], self.bfr[i]


class Builder:
    def __init__(self, layers, n_units, skip_mixer=False, skip_mlp=False, stop=None):
        self.stop = stop
        self.layers = layers
        self.skip_mixer, self.skip_mlp = skip_mixer, skip_mlp
        nc = bass.Bass("TRN2", target_bir_lowering=False)
        self.nc = nc
        dt = nc.dram_tensor
        self.xT = dt("xT", [1024, NT], F32, kind="ExternalInput")
        self.wd = dt("wstream", [n_units, 128, 2048], F32, kind="ExternalInput")
        self.cst = dt("cst", [128, CST_W], F32, kind="ExternalInput")
        self.rep = dt("rep", [128, REP_W], F32, kind="ExternalInput")
        self.oh = dt("oh", [33, OH_W], F32, kind="ExternalInput")
        self.relb = dt("relb", [32, 16], F32, kind="ExternalInput")
        self.outT = dt("outT", [1024, NT], F32, kind="ExternalOutput")
        self.kvd_in = [dt("kvd_in%d" % i, [512, 2048], BF16) for i in range(4)]
        self.kvd_out = [dt("kvd_out%d" % i, [1024, 2048], BF16) for i in range(4)]
        self.kvl_in = [dt("kvl_in%d" % i, [512, 2048], BF16) for i in range(12)]
        self.kvl_out = [dt("kvl_out%d" % i, [1024, 2048], BF16) for i in range(12)]
        self.q_loc = dt("q_loc", [3072, 2048], BF16)
        self.rep_dil = dt("rep_dil", [16 * 6 * 128, 256], F32)
        self.rep_dif = dt("rep_dif", [16 * 128, 4608], F32)
        self.tvd = dt("tvd", [16, OH_W], F32)
        self.yT_d = dt("yT_d", [2048, 2048], BF16)
        self.tail_in = dt("tail_in", [128, 128], F32)
        self.tail_out = dt("tail_out", [256, 128], F32)
        self.st_in = dt("st_in", [128, 2048], F32)
        self.st_out = dt("st_out", [256, 2048], F32)
        self.xbc_raw = dt("xbc_raw", [4096, 2048], F32)
        self.xbc_act = dt("xbc_act", [4096, 2048], F32)
        self.zs = dt("zs", [2048, 2048], F32)
        self.dres = {}

    def dr(self, name):
        if name not in self.dres:
            self.dres[name] = Res()
        return self.dres[name]

    def build(self):
        nc = self.nc
        with contextlib.ExitStack() as es:
            K = Kern(nc, es)
            self.K = K
            for i in range(8):
                t = es.enter_context(nc.psum_tensor("ps%d" % i, [128, 512], F32))
                K.banks.append(Bank(t))
            sb = lambda n, s, d: es.enter_context(nc.sbuf_tensor(n, list(s), d))
            self.hT = sb("hT", [128, 8, NT], F32)
            self.hres = mkres(4)
            self.c = sb("cst_sb", [128, CST_W], F32)
            self.cres = Res()
            self.onesb = sb("onesb", [128, 128], BF16)
            self.flagb = sb("flagb", [128, 128], BF16)
            self.identb_res = Res()
            self.misc = sb("misc", [128, 16], F32)
            self.miscres = Res()
            K.op("dve", lambda e: e.memset(self.misc[:], 0.0), writes=[self.miscres])
            self.ws = WStream(K, self.wd.ap())
            K.dma(self.c[:], self.cst.ap(), writes=[self.cres])
            for T in range(4):
                K.dma(self.hT[:, :, T * 512:(T + 1) * 512],
                      AP(self.xT, T * 512, [[NT, 128], [128 * NT, 8], [1, 512]]), writes=[self.hres[T]])
            K.op("dve", lambda e: e.tensor_copy(self.onesb[:], self.c[:, C_ONES:C_ONES + 128]),
                 reads=[self.cres], writes=[self.identb_res])
            K.op("dve", lambda e: e.tensor_scalar(out=self.flagb[:], in0=self.c[:, C_ONES:C_ONES + 128],
                                                  scalar1=self.c[:, C_FLAG:C_FLAG + 1], scalar2=None, op0=ALU.mult),
                 reads=[self.cres], writes=[self.identb_res])
            kinds = [l % 3 for l in self.layers]
            if not self.skip_mixer:
                self.setup_bias(0 in kinds, 1 in kinds)
            for l in self.layers:
                kind = l % 3
                if self.skip_mixer:
                    pass
                elif kind == 0:
                    self.dilated_layer(l)
                elif kind == 1:
                    self.diff_layer(l)
                else:
                    self.ssd_layer(l)
                if not self.skip_mlp:
                    self.mlp(l)
            self.final_norm()
            K.barrier()
        return nc

    def ones(self):
        return self.c[:, C_ONES:C_ONES + 128]

    def ident(self):
        return self.c[:, C_IDENT:C_IDENT + 128]

    def triU(self):
        return self.c[:, C_TRIU:C_TRIU + 128]

    def maskL(self):
        return self.c[:, C_MASKL:C_MASKL + 128]

    def gain(self, i, c):
        o = C_GAIN + i * 8 + c
        return self.c[:, o:o + 1]

    def flag(self):
        return self.c[:, C_FLAG:C_FLAG + 1]

    def setup_bias(self, need_dil, need_dif):
        if not (need_dil or need_dif):
            return
        K = self.K
        with K.phase() as ph:
            rb = ph.sb("rb", [33, 16], F32)
            rbr = Res()
            K.op("dve", lambda e: e.memset(rb[32:33, :], NEG), writes=[rbr])
            K.dma(rb[0:32, :], self.relb.ap(), writes=[rbr])
            ohs = ph.sb("ohs", [33, OH_W], F32)
            ohr = Res()
            K.dma(ohs[:], self.oh.ap(), writes=[ohr])
            tv = ph.sb("tv", [16, OH_W], F32)
            tvr = Res()
            nb = OH_W // 512
            for j in range(nb):
                bk = K.banks[j % 2]
                K.op("pe", lambda e: e.matmul(bk.t[0:16, :], lhsT=rb[:, :], rhs=ohs[:, j * 512:(j + 1) * 512],
                                              start=True, stop=True), reads=[rbr, ohr], writes=[bk.res])
                K.op("act", lambda e: e.copy(tv[:, j * 512:(j + 1) * 512], bk.t[0:16, :]), reads=[bk.res], writes=[tvr])
            K.dma(self.tvd.ap(), tv[:], reads=[tvr], writes=[self.dr("tvd")])
        def mk_dil(h):
            return lambda: K.dma(AP(self.rep_dil, h * 6 * 128 * 256, [[128 * 256, 6], [256, 128], [1, 256]]),
                                 AP(self.tvd, h * OH_W, [[256, 6], [0, 128], [1, 256]]),
                                 reads=[self.dr("tvd")], writes=[self.dr("rep_dil")])

        def mk_dif(h):
            return lambda: K.dma(AP(self.rep_dif, h * 128 * 4608, [[4608, 128], [1, 4608]]),
                                 AP(self.tvd, h * OH_W + 1536, [[0, 128], [1, 4608]]),
                                 reads=[self.dr("tvd")], writes=[self.dr("rep_dif")])

        if need_dil:
            K.pending += [(2, "dil", mk_dil(h)) for h in range(16)]
        if need_dif:
            K.pending += [(5, "dif", mk_dif(h)) for h in range(16)]

    def flush_pending(self, tag):
        keep = []
        for st, tg, fn in self.K.pending:
            if tg == tag:
                fn()
            else:
                keep.append((st, tg, fn))
        self.K.pending[:] = keep

    def rmsnorm(self, ph, gi, dst_fn, dst_res, src=None, srcres=None):
        K = self.K
        src = self.hT if src is None else src
        srcres = self.hres if srcres is None else srcres
        sq = ph.sb("sq", [128, 2, 512], F32)
        sqr = mkres(2)
        lnv = ph.sb("lnv", [128, 512], F32)
        rstd = ph.sb("rstd", [128, 512], F32)
        lnr, rsr = Res(), Res()
        bank = K.banks[7]
        for T in range(4):
            sl = slice(T * 512, (T + 1) * 512)
            for c in range(8):
                i = c % 2
                if c % 2 == 0:
                    K.op("pool", lambda e: e.tensor_tensor(out=sq[:, i, :], in0=src[:, c, sl], in1=src[:, c, sl], op=ALU.mult),
                         reads=[srcres[T]], writes=[sqr[i]])
                else:
                    K.op("act", lambda e: e.activation(out=sq[:, i, :], in_=src[:, c, sl], func=AF.Square), reads=[srcres[T]], writes=[sqr[i]])
                K.op("pe", lambda e: e.matmul(bank.t[:, :], lhsT=self.ones(), rhs=sq[:, i, :], start=(c == 0), stop=(c == 7)),
                     reads=[sqr[i], self.cres], writes=[bank.res])
            K.op("act", lambda e: e.activation(out=lnv[:], in_=bank.t[:, :], func=AF.Ln, scale=1.0 / 1024, bias=EPS),
                 reads=[bank.res], writes=[lnr])
            K.op("act", lambda e: e.activation(out=rstd[:], in_=lnv[:], func=AF.Exp, scale=-0.5), reads=[lnr], writes=[rsr])
            for c in range(8):
                K.op("dve", lambda e: e.scalar_tensor_tensor(out=dst_fn(c, T), in0=src[:, c, sl], scalar=self.gain(gi, c),
                                                             in1=rstd[:], op0=ALU.mult, op1=ALU.mult),
                     reads=[srcres[T], rsr, self.cres], writes=[dst_res[T]])

    def proj_F(self, src_fn, n_oc, KC, evac, rot=(0, 1, 2, 3), N=512, ntile=4):
        K = self.K
        per = 2048 // (KC * 128)
        wv = None
        bi = 0
        for oc in range(n_oc):
            s = oc % per
            if s == 0:
                w, wres = self.ws.get()
                wv = w.rearrange("p (s k j) -> p s k j", s=per, k=KC)
            for T in range(ntile):
                bank = K.banks[rot[bi % len(rot)]]
                bi += 1
                for k in range(KC):
                    ap, r = src_fn(k, T)
                    K.op("pe", lambda e: e.matmul(bank.t[:, 0:N], lhsT=wv[:, s, k, :], rhs=ap, start=(k == 0), stop=(k == KC - 1)),
                         reads=[wres, r], writes=[bank.res])
                evac(oc, T, bank)

    def proj_T(self, src_fn, n_pairs, evac, rot=(0, 1, 2, 3), ntile=16):
        K = self.K
        bi = 0
        for un in range(n_pairs):
            wa, ra = self.ws.get()
            wb, rb = self.ws.get()
            wva = wa.rearrange("p (k n) -> p k n", k=4)
            wvb = wb.rearrange("p (k n) -> p k n", k=4)
            for tt in range(ntile):
                bank = K.banks[rot[bi % len(rot)]]
                bi += 1
                for k in range(8):
                    ap, r = src_fn(k, tt)
                    wv, wr = (wva, ra) if k < 4 else (wvb, rb)
                    K.op("pe", lambda e: e.matmul(bank.t[:, :], lhsT=ap, rhs=wv[:, k % 4, :], start=(k == 0), stop=(k == 7)),
                         reads=[wr, r], writes=[bank.res])
                evac(un, tt, bank)

    def add_to_h(self, oc, T, bank):
        sl = slice(T * 512, (T + 1) * 512)
        self.K.op("dve", lambda e: e.tensor_tensor(out=self.hT[:, oc, sl], in0=bank.t[:, :], in1=self.hT[:, oc, sl], op=ALU.add),
                  reads=[bank.res, self.hres[T]], writes=[self.hres[T]])

    def mlp(self, l):
        K = self.K
        with K.phase() as ph:
            self.ws.begin(ph)
            hn = ph.sb("hn", [128, 8, NT], BF16)
            hnr = mkres(4)
            self.rmsnorm(ph, 2 * l + 1, lambda c, T: hn[:, c, T * 512:(T + 1) * 512], hnr)
            u = ph.sb("u", [128, 8, NT], BF16)
            ur = [mkres(4) for _ in range(8)]
            rl = ph.sb("rl", [128, 2, 512], F32)
            rlr = mkres(2)
            cnt = [0]

            def evac_up(oc, T, bank):
                i = cnt[0] % 2
                cnt[0] += 1
                sl = slice(T * 512, (T + 1) * 512)
                K.op("act", lambda e: e.activation(out=rl[:, i, :], in_=bank.t[:, :], func=AF.Relu), reads=[bank.res], writes=[rlr[i]])
                K.op("dve", lambda e: e.tensor_tensor(out=u[:, oc, sl], in0=rl[:, i, :], in1=rl[:, i, :], op=ALU.mult),
                     reads=[rlr[i]], writes=[ur[oc][T]])

            for G in range(4):
                self.proj_F(lambda k, T: (hn[:, k, T * 512:(T + 1) * 512], hnr[T]), 8, 8, evac_up, rot=(0, 1, 2))
                self.proj_F(lambda k, T: (u[:, k, T * 512:(T + 1) * 512], ur[k][T]), 8, 8, self.add_to_h, rot=(3, 4, 5))

    def final_norm(self):
        K = self.K
        with K.phase() as ph:
            o = ph.sb("fo", [128, 8, NT], F32)
            orr = mkres(4)
            self.rmsnorm(ph, 8, lambda c, T: o[:, c, T * 512:(T + 1) * 512], orr)
            for T in range(4):
                K.dma(AP(self.outT, T * 512, [[NT, 128], [128 * NT, 8], [1, 512]]), o[:, :, T * 512:(T + 1) * 512],
                      reads=[orr[T]], writes=[self.dr("outT")])

    def diff_layer(self, l):
        K = self.K
        li = lambda_init(l)
        with K.phase() as ph:
            qT = ph.sb("qT", [128, 8, NT], BF16)
            qr = [mkres(4) for _ in range(8)]
            with K.phase() as p1:
                self.ws.begin(p1, cast="dve")
                hn = p1.sb("hn", [128, 8, NT], BF16)
                hnr = mkres(4)
                self.rmsnorm(p1, 2 * l, lambda c, T: hn[:, c, T * 512:(T + 1) * 512], hnr)
                src = lambda k, T: (hn[:, k, T * 512:(T + 1) * 512], hnr[T])

                vst = p1.sb("vst", [128, 16, 512], BF16)
                vstr = Res()

                def evac_v(un, tt, bank):
                    K.op("act", lambda e: e.copy(vst[:, tt, :], bank.t[:, :]), reads=[bank.res], writes=[vstr])
                    if tt == 15:
                        for hf in range(2):
                            K.dma(AP(self.kvd_in[2 + hf], un * 512, [[1024, 128], [128 * 1024, 8], [1, 512]]), vst[:, hf * 8:hf * 8 + 8, :], reads=[vstr],
                                  writes=[self.dr("kvd_in%d" % (2 + hf))])

                self.proj_T(lambda k, tt: (hn[:, k, tt * 128:(tt + 1) * 128], hnr[tt // 4]), 2, evac_v)
                for i in (2, 3):
                    K.coll(self.kvd_in[i], self.kvd_out[i], reads=[self.dr("kvd_in%d" % i)], writes=[self.dr("kvd_out%d" % i)])
                kst = p1.sb("kst", [128, 2, NT], BF16)
                kstr = mkres(2)

                def evac_k(oc, T, bank):
                    i = oc % 2
                    K.op("act", lambda e: e.copy(kst[:, i, T * 512:(T + 1) * 512], bank.t[:, :]), reads=[bank.res], writes=[kstr[i]])
                    if T == 3:
                        K.dma(AP(self.kvd_in[oc // 4], (oc % 4) * 128 * 2048, [[2048, 128], [1, 2048]]), kst[:, i, :], reads=[kstr[i]],
                              writes=[self.dr("kvd_in%d" % (oc // 4))])
                        if oc % 4 == 3:
                            K.coll(self.kvd_in[oc // 4], self.kvd_out[oc // 4], reads=[self.dr("kvd_in%d" % (oc // 4))], writes=[self.dr("kvd_out%d" % (oc // 4))])

                self.proj_F(src, 8, 8, evac_k)

                def evac_q(oc, T, bank):
                    K.op("act", lambda e: e.activation(out=qT[:, oc, T * 512:(T + 1) * 512], in_=bank.t[:, :], func=AF.Copy, scale=0.125),
                         reads=[bank.res], writes=[qr[oc][T]])

                self.proj_F(src, 8, 8, evac_q)
            self.flush_pending("dif")
            if self.stop == "D1":
                return
            with K.phase() as p2:
                lv = p2.sb("lv", [128, 256], F32)
                lvr = Res()
                K.dma(lv[:], AP(self.rep, R_LAM, [[REP_W, 128], [1, 256]]), writes=[lvr])
                lt = p2.sb("lt", [128, 128], F32)
                ls = p2.sb("ls", [128, 8], F32)
                lsr = Res()
                K.op("dve", lambda e: e.tensor_tensor(out=lt[:, 0:64], in0=lv[:, 0:64], in1=lv[:, 64:128], op=ALU.mult), reads=[lvr], writes=[lsr])
                K.op("dve", lambda e: e.tensor_tensor(out=lt[:, 64:128], in0=lv[:, 128:192], in1=lv[:, 192:256], op=ALU.mult), reads=[lvr, lsr], writes=[lsr])
                K.op("dve", lambda e: e.reduce_sum(out=ls[:, 0:1], in_=lt[:, 0:64], axis=AX.X), reads=[lsr], writes=[lsr])
                K.op("dve", lambda e: e.reduce_sum(out=ls[:, 1:2], in_=lt[:, 64:128], axis=AX.X), reads=[lsr], writes=[lsr])
                K.op("act", lambda e: e.activation(out=ls[:, 2:4], in_=ls[:, 0:2], func=AF.Exp), reads=[lsr], writes=[lsr])
                K.op("dve", lambda e: e.tensor_tensor(out=ls[:, 4:5], in0=ls[:, 3:4], in1=ls[:, 2:3], op=ALU.subtract), reads=[lsr], writes=[lsr])
                K.op("dve", lambda e: e.tensor_scalar(out=ls[:, 5:6], in0=ls[:, 4:5], scalar1=-li, scalar2=None, op0=ALU.add), reads=[lsr], writes=[lsr])
                K.op("dve", lambda e: e.tensor_scalar(out=ls[:, 6:7], in0=self.c[:, C_SUBLN:C_SUBLN + 1], scalar1=1.0 - li, scalar2=None, op0=ALU.mult),
                     reads=[lsr, self.cres], writes=[lsr])
                neglam = ls[:, 5:6]
                sg = ls[:, 6:7]

                kb = p2.sb("kb", [128, 2, 2, NT], BF16)
                vb = p2.sb("vb", [128, 2, 2, 16, 128], BF16)
                kbr = mkres(2)
                vbr = mkres(2)
                SW = 2432
                strip = p2.sb("strip", [128, 2, 2, SW], F32)
                stripr = [mkres(2), mkres(2)]
                NTB, NPB, LA = 3, 6, 4
                tmp = p2.sb("tmp", [128, NTB, 512], F32)
                tmpr = mkres(NTB)
                pt = p2.sb("pt", [128, NPB, 512], BF16)
                ptr = mkres(NPB)
                rc = p2.sb("rc", [128, 2, 512], F32)
                rcr = mkres(2)
                of = p2.sb("of", [128, 512], F32)
                ofr = Res()
                osq = p2.sb("osq", [128, 512], F32)
                osr = Res()
                qz2 = p2.sb("qz2", [128, 2, 2, NT], BF16)
                qzr = mkres(2)
                K.op("pool", lambda e: e.memset(qz2[64:128, :, 0, :], 0.0), writes=qzr)
                K.op("pool", lambda e: e.memset(qz2[0:64, :, 1, :], 0.0), writes=qzr)
                jobs = []
                for h in range(8):
                    for qc in range(4):
                        tiles = [(0, kt) for kt in range(16)] + [(1, kt) for kt in range(4 * qc + 4)]
                        for ti, (slot, kt) in enumerate(tiles):
                            for m in range(2):
                                jobs.append((h, qc, m, slot, kt, ti == 0, ti == len(tiles) - 1))
                deferred = []

                def load_head(h):
                    b = h % 2
                    K.dma(kb[:, b, 0, :], AP(self.kvd_out[h // 4], (h % 4) * 128 * 2048, [[2048, 128], [1, 2048]]), reads=[self.dr("kvd_out%d" % (h // 4))], writes=[kbr[b]])
                    K.dma(kb[:, b, 1, :], AP(self.kvd_in[h // 4], (h % 4) * 128 * 2048, [[2048, 128], [1, 2048]]), reads=[self.dr("kvd_in%d" % (h // 4))], writes=[kbr[b]])
                    for hf in range(2):
                        K.dma(vb[:, b, 0, hf * 8:hf * 8 + 8, :], AP(self.kvd_out[2 + hf], h * 128, [[1024, 128], [128 * 1024, 8], [1, 128]]),
                              reads=[self.dr("kvd_out%d" % (2 + hf))], writes=[vbr[b]])
                        K.dma(vb[:, b, 1, hf * 8:hf * 8 + 8, :], AP(self.kvd_in[2 + hf], h * 128, [[1024, 128], [128 * 1024, 8], [1, 128]]),
                              reads=[self.dr("kvd_in%d" % (2 + hf))], writes=[vbr[b]])
                    K.op("dve", lambda e: e.tensor_scalar(out=vb[:, b, 0, :, :], in0=vb[:, b, 0, :, :], scalar1=self.flag(), scalar2=None, op0=ALU.mult),
                         reads=[vbr[b], self.cres], writes=[vbr[b]])
                    K.op("pool", lambda e: e.tensor_copy(qz2[0:64, b, 0, :], qT[0:64, h, :]), reads=qr[h], writes=[qzr[b]])
                    K.op("pool", lambda e: e.tensor_copy(qz2[64:128, b, 1, :], qT[64:128, h, :]), reads=qr[h], writes=[qzr[b]])

                def load_strips(h):
                    for m in range(2):
                        col = m * 8 + h
                        K.dma(strip[:, h % 2, m, :], AP(self.rep_dif, col * 128 * 4608 + 127, [[4607, 128], [1, SW]]),
                              reads=[self.dr("rep_dif")], writes=[stripr[h % 2][m]])

                def emit_score(idx):
                    h, qc, m, slot, kt, first, last = jobs[idx]
                    b = h % 2
                    rows = slice(m * 64, (m + 1) * 64)
                    qsl = slice(qc * 512, (qc + 1) * 512)
                    G = (2048 if slot == 0 else 0) + qc * 512 - kt * 128
                    y0 = G + 384
                    sb_ = K.banks[idx % 3]
                    ip = idx % NPB
                    K.op("pe", lambda e: e.matmul(sb_.t[:, :], lhsT=kb[:, b, slot, kt * 128:(kt + 1) * 128], rhs=qz2[:, b, m, qsl],
                                                  start=True, stop=True), reads=[kbr[b], qzr[b]], writes=[sb_.res])
                    if G - 127 >= 1512:
                        K.op("act", lambda e: e.activation(out=pt[:, ip, :], in_=sb_.t[:, :], func=AF.Exp, bias=strip[:, b, m, SW - 1:SW], scale=1.0),
                             reads=[sb_.res, stripr[b][m]], writes=[ptr[ip]])
                    else:
                        it_ = idx % NTB
                        K.op("dve", lambda e: e.scalar_tensor_tensor(out=tmp[:, it_, :], in0=sb_.t[:, :], scalar=60.0, in1=strip[:, b, m, y0:y0 + 512],
                                                                     op0=ALU.min, op1=ALU.add), reads=[sb_.res, stripr[b][m]], writes=[tmpr[it_]])
                        K.op("act", lambda e: e.activation(out=pt[:, ip, :], in_=tmp[:, it_, :], func=AF.Exp), reads=[tmpr[it_]], writes=[ptr[ip]])

                def epi2(h, qc):
                    qsl = slice(qc * 512, (qc + 1) * 512)
                    sbk = K.banks[7]
                    K.op("pe", lambda e: e.matmul(sbk.t[:, :], lhsT=self.ones(), rhs=osq[:], start=True, stop=True), reads=[osr, self.cres], writes=[sbk.res])
                    K.op("act", lambda e: e.activation(out=osq[:], in_=sbk.t[:, :], func=AF.Ln, scale=1.0 / 128, bias=EPS), reads=[sbk.res], writes=[osr])
                    K.op("act", lambda e: e.activation(out=osq[:], in_=osq[:], func=AF.Exp, scale=-0.5), reads=[osr], writes=[osr])
                    K.op("dve", lambda e: e.scalar_tensor_tensor(out=qT[:, h, qsl], in0=of[:], scalar=sg, in1=osq[:], op0=ALU.mult, op1=ALU.mult),
                         reads=[ofr, osr, lsr], writes=[qr[h][qc]])

                def emit_pv(idx):
                    h, qc, m, slot, kt, first, last = jobs[idx]
                    b = h % 2
                    ip = idx % NPB
                    numb = K.banks[3 + 2 * m]
                    denb = K.banks[4 + 2 * m]
                    K.op("pe", lambda e: e.matmul(numb.t[:, :], lhsT=vb[:, b, slot, kt, :], rhs=pt[:, ip, :], start=first, stop=last),
                         reads=[vbr[b], ptr[ip]], writes=[numb.res])
                    dl = self.flagb if slot == 0 else self.onesb
                    K.op("pe", lambda e: e.matmul(denb.t[:, :], lhsT=dl[:], rhs=pt[:, ip, :], start=first, stop=last),
                         reads=[self.identb_res, ptr[ip]], writes=[denb.res])
                    if last:
                        K.op("act", lambda e: e.activation(out=rc[:, m, :], in_=denb.t[:, :], func=AF.Ln), reads=[denb.res], writes=[rcr[m]])
                        K.op("act", lambda e: e.activation(out=rc[:, m, :], in_=rc[:, m, :], func=AF.Exp, scale=-1.0), reads=[rcr[m]], writes=[rcr[m]])
                        K.op("dve", lambda e: e.tensor_tensor(out=rc[:, m, :], in0=numb.t[:, :], in1=rc[:, m, :], op=ALU.mult),
                             reads=[numb.res, rcr[m]], writes=[rcr[m]])
                        if m == 1:
                            K.op("dve", lambda e: e.scalar_tensor_tensor(out=of[:], in0=rc[:, 1, :], scalar=neglam, in1=rc[:, 0, :], op0=ALU.mult, op1=ALU.add),
                                 reads=[rcr[0], rcr[1], lsr], writes=[ofr])
                            K.op("pool", lambda e: e.tensor_tensor(out=osq[:], in0=of[:], in1=of[:], op=ALU.mult), reads=[ofr], writes=[osr])
                            deferred.append((idx + LA + 10, lambda: epi2(h, qc)))

                nj = len(jobs)
                for h0 in range(2):
                    load_head(h0)
                    load_strips(h0)
                for idx in range(nj + LA):
                    if idx < nj:
                        emit_score(idx)
                    if idx - LA >= 0:
                        emit_pv(idx - LA)
                        jh = jobs[idx - LA]
                        if jh[1] == 3 and jh[2] == 1 and jh[6] and jh[0] + 2 < 8:
                            load_head(jh[0] + 2)
                            load_strips(jh[0] + 2)
                    while deferred and deferred[0][0] <= idx:
                        deferred.pop(0)[1]()
                while deferred:
                    deferred.pop(0)[1]()
            with K.phase() as p3:
                self.ws.begin(p3)
                self.proj_F(lambda k, T: (qT[:, k, T * 512:(T + 1) * 512], qr[k][T]), 8, 8, self.add_to_h)

    def regroup_ap(self, t, k, D, T):
        base = k * NT
        ps = pstep(t.tensor if hasattr(t, "tensor") else t)
        th = t.tensor if hasattr(t, "tensor") else t
        if D == 1:
            return AP(th, base + 512 * T, [[ps, 128], [1, 512]])
        if D == 4:
            return AP(th, base + T, [[ps, 128], [4, 512]])
        return AP(th, base + 4 * T, [[ps, 128], [1, 4], [16, 128]])

    def dilated_layer(self, l):
        K = self.K
        with K.phase() as ph:
            with K.phase() as p1:
                self.ws.begin(p1, cast="dve")
                hn = p1.sb("hn", [128, 8, NT], BF16)
                hnr = mkres(4)
                allr = Res()
                self.rmsnorm(p1, 2 * l, lambda c, T: hn[:, c, T * 512:(T + 1) * 512], hnr)
                K.op("pool", lambda e: e.tensor_copy(self.misc[:, 0:1], self.misc[:, 1:2]), reads=hnr + [self.miscres], writes=[allr, self.miscres])
                st = p1.sb("qkst", [128, 2, NT], BF16)
                str_ = mkres(2)
                vst = p1.sb("vst", [128, 4, 512], BF16)
                vstr4 = mkres(4)
                hps = pstep(hn)
                for g, D in enumerate(DILS):
                    L = NT // D
                    src = lambda k, T: (self.regroup_ap(hn, k, D, T), allr)

                    def qk_proj(which, scale):
                        def evac_qk(oc, T, bank):
                            i = oc % 2
                            if D == 1:
                                o_ap, i_ap = st[:, i, T * 512:(T + 1) * 512], bank.t[:, :]
                            elif D == 4:
                                o_ap = AP(st, i * NT + 128 * T, [[2 * NT, 128], [512, 4], [1, 128]])
                                i_ap = bank.t[:, :].rearrange("p (a r) -> p r a", r=4)
                            else:
                                o_ap = AP(st, i * NT + 32 * T, [[2 * NT, 128], [128, 16], [1, 32]])
                                i_ap = bank.t[:, :].rearrange("p (a r) -> p r a", r=16)
                            K.op("act", lambda e: e.activation(out=o_ap, in_=i_ap, func=AF.Copy, scale=scale),
                                 reads=[bank.res], writes=[str_[i]])
                            if T == 3:
                                if which == 0:
                                    K.dma(AP(self.q_loc, (g * 1024 + oc * 128) * 2048, [[2048, 128], [1, 2048]]), st[:, i, :], reads=[str_[i]],
                                          writes=[self.dr("q_loc")])
                                else:
                                    ch = g * 4 + oc // 4
                                    K.dma(AP(self.kvl_in[ch], (oc % 4) * 128 * 2048, [[2048, 128], [1, 2048]]), st[:, i, :], reads=[str_[i]],
                                          writes=[self.dr("kvl_in%d" % ch)])
                                    if oc % 4 == 3:
                                        K.coll(self.kvl_in[ch], self.kvl_out[ch], reads=[self.dr("kvl_in%d" % ch)], writes=[self.dr("kvl_out%d" % ch)])

                        self.proj_F(lambda k, T: (hn[:, k, T * 512:(T + 1) * 512], hnr[T]), 8, 8, evac_qk)

                    def vsrc(k, tt):
                        return hn[:, k, tt * 128:(tt + 1) * 128], hnr[tt // 4]

                    def evac_v(un, tt, bank):
                        i = tt % 4
                        K.op("act", lambda e: e.copy(vst[:, i, :], bank.t[:, :]), reads=[bank.res], writes=[vstr4[i]])
                        ch = g * 4 + 2 + un
                        dst = AP(self.kvl_in[ch], (128 * tt // D) * 512, [[512, 128 // D], [L * 512, D], [1, 512]])
                        K.dma(dst, vst[:, i, :], reads=[vstr4[i]], writes=[self.dr("kvl_in%d" % ch)])

                    qk_proj(1, 1.0)
                    self.proj_T(vsrc, 2, evac_v)
                    for hf in range(2):
                        ch = g * 4 + 2 + hf
                        K.coll(self.kvl_in[ch], self.kvl_out[ch], reads=[self.dr("kvl_in%d" % ch)], writes=[self.dr("kvl_out%d" % ch)])
                    qk_proj(0, 0.125)
            self.flush_pending("dil")
            oT = ph.sb("oT", [128, 8, NT], BF16)
            otr = [mkres(4) for _ in range(8)]
            with K.phase() as p2:
                qz = p2.sb("qz", [128, 2, 2, NT], BF16)
                ko = p2.sb("ko", [128, 2, NT], BF16)
                kh = p2.sb("kh", [128, 2, 16, 128], BF16)
                vo = p2.sb("vo", [128, 1, 16, 128], BF16)
                vh = p2.sb("vh", [128, 1, 16, 128], BF16)
                voz = p2.sb("voz", [128, 2, 2, 16, 128], BF16)
                vhz = p2.sb("vhz", [128, 2, 2, 16, 128], BF16)
                oz = p2.sb("oz", [128, 2, 128], BF16)
                fz = p2.sb("fz", [128, 2, 128], BF16)
                ozr = Res()
                bt = p2.sb("bt", [128, 2, 2, 2, 128], F32)
                ldr = mkres(2)
                vzr = mkres(2)
                vldr = Res()
                K.op("pool", lambda e: e.memset(qz[64:128, :, 0, :], 0.0), writes=ldr)
                K.op("pool", lambda e: e.memset(qz[0:64, :, 1, :], 0.0), writes=ldr)
                K.op("pool", lambda e: e.memset(voz[:], 0.0), writes=vzr)
                K.op("pool", lambda e: e.memset(vhz[:], 0.0), writes=vzr)
                K.op("dve", lambda e: e.memset(oz[:], 0.0), writes=[ozr])
                K.op("dve", lambda e: e.memset(fz[:], 0.0), writes=[ozr])
                for hh_ in range(2):
                    K.op("dve", lambda e: e.tensor_copy(oz[:, hh_, hh_ * 64:hh_ * 64 + 64], self.onesb[:, 0:64]), reads=[self.identb_res, ozr], writes=[ozr])
                    K.op("dve", lambda e: e.tensor_copy(fz[:, hh_, hh_ * 64:hh_ * 64 + 64], self.flagb[:, 0:64]), reads=[self.identb_res, ozr], writes=[ozr])
                accn = p2.sb("accn", [128, NT], F32)
                accd = p2.sb("accd", [128, NT], F32)
                accr = Res()
                NTB, NPB, LA = 3, 6, 4
                tmp = p2.sb("tmp", [128, NTB, 512], F32)
                tmpr = mkres(NTB)
                pt = p2.sb("pt", [128, NPB, 512], BF16)
                ptr = mkres(NPB)
                combos = [(p, g) for p in range(8) for g in range(3)]
                jobs = [(ci, qgp, hh, half) for ci in range(len(combos)) for qgp in range(4) for hh in range(2) for half in range(2)]
                accp = [accr]

                def load_pg(ci):
                    p, g = combos[ci]
                    D = DILS[g]
                    b = ci % 2
                    L = NT // D
                    bpr = 16 // D
                    kc_in, kc_out = self.kvl_in[g * 4 + p // 4], self.kvl_out[g * 4 + p // 4]
                    krow = (p % 4) * 128 * 2048
                    K.dma(qz[0:64, b, 0, :], AP(self.q_loc, (g * 1024 + p * 128) * 2048, [[2048, 64], [1, 2048]]), reads=[self.dr("q_loc")], writes=[ldr[b]])
                    K.dma(qz[64:128, b, 1, :], AP(self.q_loc, (g * 1024 + p * 128 + 64) * 2048, [[2048, 64], [1, 2048]]), reads=[self.dr("q_loc")], writes=[ldr[b]])
                    K.dma(ko[:, b, :], AP(kc_in, krow, [[2048, 128], [1, 2048]]), reads=[self.dr("kvl_in%d" % (g * 4 + p // 4))], writes=[ldr[b]])
                    K.dma(kh[:, b, 0:D, :], AP(kc_out, krow + L - 128, [[2048, 128], [L, D], [1, 128]]),
                          reads=[self.dr("kvl_out%d" % (g * 4 + p // 4))], writes=[ldr[b]])
                    vch = g * 4 + 2 + p // 4
                    vcol = (p % 4) * 128
                    K.dma(vo[:, 0, :, :], AP(self.kvl_in[vch], vcol, [[512, 128], [128 * 512, 16], [1, 128]]),
                          reads=[self.dr("kvl_in%d" % vch)], writes=[vldr])
                    K.dma(vh[:, 0, 0:D, :], AP(self.kvl_out[vch], vcol + (bpr - 1) * 128 * 512, [[512, 128], [bpr * 128 * 512, D], [1, 128]]),
                          reads=[self.dr("kvl_out%d" % vch)], writes=[vldr])
                    for hh_ in range(2):
                        K.dma(bt[:, b, hh_, :, :], AP(self.rep_dil, ((2 * p + hh_) * 6 + 2 * g) * 128 * 256 + 127, [[255, 128], [128 * 256, 2], [1, 128]]),
                              reads=[self.dr("rep_dil")], writes=[ldr[b]])
                    K.op("dve", lambda e: e.tensor_scalar(out=vh[:, 0, 0:D, :], in0=vh[:, 0, 0:D, :], scalar1=self.flag(), scalar2=None, op0=ALU.mult),
                         reads=[vldr, self.cres], writes=[vldr])
                    for hh_ in range(2):
                        fs = slice(hh_ * 64, hh_ * 64 + 64)
                        K.op("pool", lambda e: e.tensor_copy(voz[:, b, hh_, :, fs], vo[:, 0, :, fs]), reads=[vldr], writes=[vzr[b]])
                        K.op("pool", lambda e: e.tensor_copy(vhz[:, b, hh_, 0:D, fs], vh[:, 0, 0:D, fs]), reads=[vldr], writes=[vzr[b]])

                def blocks_of(idx):
                    ci, qgp, hh, half = jobs[idx]
                    p, g = combos[ci]
                    D = DILS[g]
                    b = ci % 2
                    bpr = 16 // D
                    rows = slice(hh * 64, (hh + 1) * 64)
                    out = []
                    for j in range(2):
                        blk = qgp * 4 + half * 2 + j
                        if blk % bpr == 0:
                            kprev = kh[:, b, blk // bpr, :]
                            vprev = vhz[:, b, hh, blk // bpr, :]
                            dprev = fz[:, hh, :]
                        else:
                            kprev = ko[:, b, (blk - 1) * 128:blk * 128]
                            vprev = voz[:, b, hh, blk - 1, :]
                            dprev = oz[:, hh, :]
                        kcur = ko[:, b, blk * 128:(blk + 1) * 128]
                        vcur = voz[:, b, hh, blk, :]
                        out.append((blk, kprev, kcur, vprev, dprev, vcur))
                    return out

                def emit_score(idx):
                    ci, qgp, hh, half = jobs[idx]
                    b = ci % 2
                    rows = slice(hh * 64, (hh + 1) * 64)
                    sb_ = K.banks[idx % 3]
                    it_, ip = idx % NTB, idx % NPB
                    for j, (blk, kprev, kcur, vprev, dprev, vcur) in enumerate(blocks_of(idx)):
                        qap = qz[:, b, hh, blk * 128:(blk + 1) * 128]
                        K.op("pe", lambda e: e.matmul(sb_.t[:, j * 256:j * 256 + 128], lhsT=kprev, rhs=qap, start=True, stop=True),
                             reads=[ldr[b]], writes=[sb_.res])
                        K.op("pe", lambda e: e.matmul(sb_.t[:, j * 256 + 128:j * 256 + 256], lhsT=kcur, rhs=qap, start=True, stop=True),
                             reads=[ldr[b]], writes=[sb_.res])
                    bias2 = AP(bt, (b * 2 + hh) * 256, [[pstep(bt), 128], [0, 2], [1, 256]])
                    K.op("dve", lambda e: e.scalar_tensor_tensor(out=tmp[:, it_, :].rearrange("p (a c) -> p a c", a=2),
                                                                 in0=sb_.t[:, :].rearrange("p (a c) -> p a c", a=2), scalar=60.0, in1=bias2,
                                                                 op0=ALU.min, op1=ALU.add), reads=[sb_.res, ldr[b]], writes=[tmpr[it_]])
                    K.op("act", lambda e: e.activation(out=pt[:, ip, :], in_=tmp[:, it_, :], func=AF.Exp), reads=[tmpr[it_]], writes=[ptr[ip]])

                def emit_pv(idx):
                    ci, qgp, hh, half = jobs[idx]
                    p, g = combos[ci]
                    D = DILS[g]
                    b = ci % 2
                    ip = idx % NPB
                    rows = slice(hh * 64, (hh + 1) * 64)
                    numb = K.banks[3 + 2 * (qgp % 2)]
                    denb = K.banks[4 + 2 * (qgp % 2)]
                    for j, (blk, kprev, kcur, vprev, dprev, vcur) in enumerate(blocks_of(idx)):
                        cs = slice((blk % 4) * 128, (blk % 4) * 128 + 128)
                        pprev = pt[:, ip, j * 256:j * 256 + 128]
                        pcur = pt[:, ip, j * 256 + 128:j * 256 + 256]
                        st = (hh == 0 and half == 0 and j == 0)
                        fin = (hh == 1 and half == 1 and j == 1)
                        K.op("pe", lambda e: e.matmul(numb.t[:, cs], lhsT=vprev, rhs=pprev, start=st, stop=False, skip_group_check=True), reads=[vzr[b], ptr[ip]], writes=[numb.res])
                        K.op("pe", lambda e: e.matmul(numb.t[:, cs], lhsT=vcur, rhs=pcur, start=False, stop=fin, skip_group_check=True), reads=[vzr[b], ptr[ip]], writes=[numb.res])
                        K.op("pe", lambda e: e.matmul(denb.t[:, cs], lhsT=dprev, rhs=pprev, start=st, stop=False, skip_group_check=True),
                             reads=[ozr, ptr[ip]], writes=[denb.res])
                        K.op("pe", lambda e: e.matmul(denb.t[:, cs], lhsT=oz[:, hh, :], rhs=pcur, start=False, stop=fin, skip_group_check=True),
                             reads=[ozr, ptr[ip]], writes=[denb.res])
                    if not (hh == 1 and half == 1):
                        return
                    if D == 1:
                        an, ad = accn[:, qgp * 512:(qgp + 1) * 512], accd[:, qgp * 512:(qgp + 1) * 512]
                        sn, sd = numb.t[:, :], denb.t[:, :]
                    elif D == 4:
                        an = AP(accn, qgp, [[NT, 128], [4, 512]])
                        ad = AP(accd, qgp, [[NT, 128], [4, 512]])
                        sn, sd = numb.t[:, :], denb.t[:, :]
                    else:
                        an = AP(accn, 4 * qgp, [[NT, 128], [1, 4], [16, 128]])
                        ad = AP(accd, 4 * qgp, [[NT, 128], [1, 4], [16, 128]])
                        sn = numb.t[:, :].rearrange("p (a c) -> p a c", a=4)
                        sd = denb.t[:, :].rearrange("p (a c) -> p a c", a=4)
                    if g == 0:
                        K.op("act", lambda e: e.copy(an, sn), reads=[numb.res], writes=[accr])
                        K.op("act", lambda e: e.copy(ad, sd), reads=[denb.res], writes=[accr])
                    else:
                        K.op("dve", lambda e: e.tensor_tensor(out=an, in0=sn, in1=an, op=ALU.add), reads=[numb.res, accr], writes=[accr])
                        K.op("dve", lambda e: e.tensor_tensor(out=ad, in0=sd, in1=ad, op=ALU.add), reads=[denb.res, accr], writes=[accr])
                    if g == 2 and qgp == 3:
                        K.op("act", lambda e: e.activation(out=accd[:], in_=accd[:], func=AF.Ln), reads=[accr], writes=[accr])
                        K.op("act", lambda e: e.activation(out=accd[:], in_=accd[:], func=AF.Exp, scale=-1.0), reads=[accr], writes=[accr])
                        K.op("pool", lambda e: e.tensor_tensor(out=oT[:, p, :], in0=accn[:], in1=accd[:], op=ALU.mult), reads=[accr], writes=otr[p])

                nj = len(jobs)
                load_pg(0)
                load_pg(1)
                for idx in range(nj + LA):
                    if idx < nj:
                        emit_score(idx)
                    if idx - LA >= 0:
                        emit_pv(idx - LA)
                        jc = jobs[idx - LA]
                        if jc[1] == 3 and jc[2] == 1 and jc[3] == 1 and jc[0] + 2 < len(combos):
                            load_pg(jc[0] + 2)
            with K.phase() as p3:
                self.ws.begin(p3)
                self.proj_F(lambda k, T: (oT[:, k, T * 512:(T + 1) * 512], otr[k][T]), 8, 8, self.add_to_h)

    def ssd_layer(self, l):
        K = self.K
        sl4 = lambda T: slice(T * 512, (T + 1) * 512)
        with K.phase() as ph:
            dtT = ph.sb("dtT", [128, 16, 32], F32)
            aT = ph.sb("aT", [128, 16, 32], F32)
            dar = Res()
            rawh = ph.sb("rawh", [128, 32, 8], F32)
            rawr = Res()
            fx = ph.sb("fx", [128, 32, 3], F32)
            fxr = Res()
            self.fx, self.fxr = fx, fxr
            with K.phase() as p1:
                self.ws.begin(p1)
                hn = p1.sb("hn", [128, 8, NT], BF16)
                hnr = mkres(4)
                self.rmsnorm(p1, 2 * l, lambda c, T: hn[:, c, sl4(T)], hnr)
                src = lambda k, T: (hn[:, k, sl4(T)], hnr[T])
                zst = p1.sb("zst", [128, 8, 512], F32)
                zr = Res()

                def evac_z(un, tt, bank):
                    K.op("act", lambda e: e.activation(out=zst[:, tt % 8, :], in_=bank.t[:, :], func=AF.Silu), reads=[bank.res], writes=[zr])
                    if tt % 8 == 7:
                        K.dma(AP(self.zs, (tt - 7) * 128 * 2048 + un * 512, [[2048, 128], [128 * 2048, 8], [1, 512]]), zst[:], reads=[zr], writes=[self.dr("zs")])

                self.proj_T(lambda k, tt: (hn[:, k, tt * 128:(tt + 1) * 128], hnr[tt // 4]), 4, evac_z)
                xb = p1.sb("xb", [128, 2, NT + 4], F32)
                xr = mkres(2)
                cacc = p1.sb("cacc", [128, 2, NT], F32)
                caccr = mkres(2)
                tl = p1.sb("tl", [128, 32, 4], F32)
                tlr = Res()
                K.op("dve", lambda e: e.memset(tl[:], 0.0), writes=[tlr])
                K.op("dve", lambda e: e.memset(xb[:, :, 0:3], 0.0), writes=xr)

                def evac_x(oc, T, bank):
                    i = oc % 2
                    K.op("act", lambda e: e.copy(xb[:, i, 3 + T * 512:3 + (T + 1) * 512], bank.t[:, :]), reads=[bank.res], writes=[xr[i]])
                    if T == 3:
                        K.op("dve", lambda e: e.tensor_copy(tl[:, oc, 0:3], xb[:, i, NT:NT + 3]), reads=[xr[i]], writes=[tlr])
                        K.op("dve", lambda e: e.tensor_copy(rawh[:, oc, 3:6], xb[:, i, 3:6]), reads=[xr[i]], writes=[rawr])
                        wj = lambda j: self.c[:, C_CONVW + oc * 4 + j:C_CONVW + oc * 4 + j + 1]
                        K.op("act", lambda e: e.activation(out=cacc[:, i, :], in_=xb[:, i, 0:NT], func=AF.Identity, scale=wj(0),
                                                           bias=self.c[:, C_CONVB + oc:C_CONVB + oc + 1]), reads=[xr[i], self.cres], writes=[caccr[i]])
                        for j in range(1, 4):
                            K.op("dve", lambda e: e.scalar_tensor_tensor(out=cacc[:, i, :], in0=xb[:, i, j:j + NT], scalar=wj(j), in1=cacc[:, i, :], op0=ALU.mult, op1=ALU.add),
                                 reads=[xr[i], caccr[i], self.cres], writes=[caccr[i]])
                        K.op("act", lambda e: e.activation(out=cacc[:, i, :], in_=cacc[:, i, :], func=AF.Silu), reads=[caccr[i]], writes=[caccr[i]])
                        K.dma(AP(self.xbc_act, oc * 128 * 2048, [[2048, 128], [1, 2048]]), cacc[:, i, :], reads=[caccr[i]], writes=[self.dr("xbc_act")])

                self.proj_F(src, 32, 8, evac_x)
                K.dma(self.tail_in.ap(), tl[:].rearrange("p a b -> p (a b)"), reads=[tlr], writes=[self.dr("tail_in")])
                K.coll(self.tail_in, self.tail_out, reads=[self.dr("tail_in")], writes=[self.dr("tail_out")])
                dtf = cacc[0:32, 0, :]
                af = cacc[0:32, 1, :]
                dfr = Res()
                K.op("dve", lambda e: e.tensor_copy(self.misc[:, 2:3], self.misc[:, 3:4]), reads=[self.miscres], writes=[dfr, self.miscres] + caccr)
                Ac = p1.sb("Ac", [32, 2], F32)
                Ar = Res()
                K.op("act", lambda e: e.activation(out=Ac[:, 0:1], in_=self.c[0:32, C_ALOG:C_ALOG + 1], func=AF.Exp), reads=[self.cres], writes=[Ar])
                K.op("dve", lambda e: e.tensor_scalar(out=Ac[:, 1:2], in0=Ac[:, 0:1], scalar1=-1.0, scalar2=None, op0=ALU.mult), reads=[Ar], writes=[Ar])

                def evac_dt(oc, T, bank):
                    K.op("act", lambda e: e.activation(out=dtf[:, sl4(T)], in_=bank.t[0:32, :], func=AF.Exp, bias=self.c[0:32, C_DTB:C_DTB + 1], scale=1.0),
                         reads=[bank.res, self.cres], writes=[dfr])
                    K.op("act", lambda e: e.activation(out=dtf[:, sl4(T)], in_=dtf[:, sl4(T)], func=AF.Ln, bias=1.0, scale=1.0), reads=[dfr], writes=[dfr])
                    K.op("dve", lambda e: e.tensor_scalar(out=af[:, sl4(T)], in0=dtf[:, sl4(T)], scalar1=Ac[:, 1:2], scalar2=None, op0=ALU.mult),
                         reads=[dfr, Ar], writes=[dfr])

                self.proj_F(src, 1, 8, evac_dt)
                for c in range(16):
                    bank = K.banks[4 + c % 2]
                    K.op("pe", lambda e: e.transpose(out=bank.t[:, 0:32], in_=dtf[:, c * 128:(c + 1) * 128], identity=self.c[0:32, C_IDENT:C_IDENT + 32]),
                         reads=[dfr, self.cres], writes=[bank.res])
                    K.op("pe", lambda e: e.transpose(out=bank.t[:, 32:64], in_=af[:, c * 128:(c + 1) * 128], identity=self.c[0:32, C_IDENT:C_IDENT + 32]),
                         reads=[dfr, self.cres], writes=[bank.res])
                    K.op("act", lambda e: e.copy(dtT[:, c, :], bank.t[:, 0:32]), reads=[bank.res], writes=[dar])
                    K.op("act", lambda e: e.copy(aT[:, c, :], bank.t[:, 32:64]), reads=[bank.res], writes=[dar])
            with K.phase() as p2:
                K.dma(rawh[:, :, 0:3], AP(self.tail_out, 0, [[128, 128], [4, 32], [1, 3]]), reads=[self.dr("tail_out")], writes=[rawr])
                K.op("dve", lambda e: e.tensor_scalar(out=rawh[:, :, 0:3], in0=rawh[:, :, 0:3], scalar1=self.flag(), scalar2=None, op0=ALU.mult),
                     reads=[rawr, self.cres], writes=[rawr])
                tf = p2.sb("tf", [128, 32, 3], F32)
                wv = lambda j: AP(self.c, C_CONVW + j, [[CST_W, 128], [4, 32], [0, 3]])
                K.op("dve", lambda e: e.tensor_tensor(out=fx[:], in0=rawh[:, :, 0:3], in1=wv(0), op=ALU.mult), reads=[rawr, self.cres], writes=[fxr])
                K.op("dve", lambda e: e.tensor_tensor(out=fx[:], in0=fx[:], in1=AP(self.c, C_CONVB, [[CST_W, 128], [1, 32], [0, 3]]), op=ALU.add),
                     reads=[fxr, self.cres], writes=[fxr])
                for j in range(1, 4):
                    K.op("dve", lambda e: e.tensor_tensor(out=tf[:], in0=rawh[:, :, j:j + 3], in1=wv(j), op=ALU.mult), reads=[rawr, self.cres, fxr], writes=[fxr])
                    K.op("dve", lambda e: e.tensor_tensor(out=fx[:], in0=fx[:], in1=tf[:], op=ALU.add), reads=[fxr], writes=[fxr])
                K.op("act", lambda e: e.activation(out=fx[:], in_=fx[:], func=AF.Silu), reads=[fxr], writes=[fxr])
            for states_only in (True, False):
                with K.phase() as p3:
                    self.ssd_scan(p3, dtT, aT, dar, states_only)
            with K.phase() as p4:
                self.ws.begin(p4)
                ysb = p4.sb("ysb", [128, 16, NT], BF16)
                ysr = mkres(4)
                for T in range(4):
                    K.dma(ysb[:, :, sl4(T)], AP(self.yT_d, T * 512, [[2048, 128], [128 * 2048, 16], [1, 512]]), reads=[self.dr("yT_d")], writes=[ysr[T]])
                self.proj_F(lambda k, T: (ysb[:, k, sl4(T)], ysr[T]), 8, 16, self.add_to_h)

    def ssd_scan(self, ph, dtT, aT, dar, states_only):
        K = self.K
        sb = ph.sb
        stT = sb("stT", [128, 2048], F32)
        strg = mkres(8)
        prevb = sb("prevb", [128, 2048], BF16)
        pvrg = mkres(8)
        if states_only:
            K.op("dve", lambda e: e.memset(stT[:], 0.0), writes=strg)
        else:
            K.dma(stT[:], AP(self.st_out, 0, [[2048, 128], [1, 2048]]), reads=[self.dr("st_out")], writes=strg)
            K.op("dve", lambda e: e.tensor_scalar(out=stT[:], in0=stT[:], scalar1=self.flag(), scalar2=None, op0=ALU.mult), reads=strg + [self.cres], writes=strg)
            K.op("pool", lambda e: e.tensor_copy(prevb[:], stT[:]), reads=strg, writes=pvrg)
        nfc = 24 if states_only else 32
        xa2 = sb("xa", [128, 2, 32, 128], F32)
        xar2 = mkres(2)
        x_tm = sb("x_tm", [128, 2048], F32)
        xtr = Res()
        xdt = sb("xdt", [128, 2048], BF16)
        xw = sb("xw", [128, 2048], BF16)
        xdr, xwr = Res(), Res()
        Btm = sb("Btm", [128, 8, 128], BF16)
        btr = Res()
        sm = sb("sm", [128, 6, 32], F32)
        smr = Res()
        if not states_only:
            zc2 = sb("zc", [128, 2, 2048], F32)
            zcr2 = mkres(2)
            bcb = sb("bcb", [128, 16, 128], BF16)
            bcr = Res()
            cbm2 = sb("cbm", [128, 2, 128], F32)
            cbr2 = mkres(2)
            La2 = sb("La", [128, 2, 4, 128], F32)
            lar2 = mkres(2)
            dec2 = sb("dec", [128, 2, 4, 128], F32)
            der2 = mkres(2)
            MT2 = sb("MT", [128, 2, 4, 128], BF16)
            mtr2 = mkres(2)
            yo = sb("yo", [128, 512], F32)
            yor = Res()
            cbres = mkres(2)
            y = sb("y", [128, 2048], F32)
            yr = Res()
            junk = sb("junk", [128, 256], F32)
            ss = sb("ss", [128, 16], F32)
            ssr = Res()
            gg = sb("gg", [128, 2048], F32)
            Dh = sb("Dh", [128, 32], F32)
            ggr = Res()
            K.dma(gg[:], AP(self.rep, R_GG, [[REP_W, 128], [1, 2048]]), writes=[ggr])
            K.dma(Dh[:], AP(self.rep, R_DH, [[REP_W, 128], [1, 32]]), writes=[ggr])
            yTc = sb("yTc", [128, 16, 128], BF16)
            ytr = Res()
        ones, triU, maskL, ident = self.ones(), self.triU(), self.maskL(), self.ident()
        def load_chunk(c):
            K.dma(xa2[:, c % 2, 0:nfc, :], AP(self.xbc_act, c * 128, [[2048, 128], [128 * 2048, nfc], [1, 128]]), reads=[self.dr("xbc_act")], writes=[xar2[c % 2]])
            if not states_only:
                K.dma(zc2[:, c % 2, :], AP(self.zs, c * 128 * 2048, [[2048, 128], [1, 2048]]), reads=[self.dr("zs")], writes=[zcr2[c % 2]])

        load_chunk(0)
        K.op("dve", lambda e: e.tensor_copy(xa2[:, 0, 0:nfc, 0:3], self.fx[:, 0:nfc, :]), reads=[self.fxr, xar2[0]], writes=[xar2[0]])
        for c in range(16):
            if c + 1 < 16:
                load_chunk(c + 1)
            xa, xar = xa2[:, c % 2, :, :], xar2[c % 2]
            if not states_only:
                zc, zcr = zc2[:, c % 2, :], zcr2[c % 2]
            b6 = K.banks[6]
            K.op("pe", lambda e: e.matmul(b6.t[:, 0:32], lhsT=triU, rhs=aT[:, c, :], start=True, stop=True), reads=[dar, self.cres], writes=[b6.res])
            K.op("pe", lambda e: e.matmul(b6.t[:, 32:64], lhsT=ones, rhs=aT[:, c, :], start=True, stop=True), reads=[dar, self.cres], writes=[b6.res])
            K.op("act", lambda e: e.copy(sm[:, 0:2, :].rearrange("p a b -> p (a b)"), b6.t[:, 0:64]), reads=[b6.res], writes=[smr])
            K.op("dve", lambda e: e.tensor_tensor(out=sm[:, 2, :], in0=sm[:, 1, :], in1=sm[:, 0, :], op=ALU.subtract), reads=[smr], writes=[smr])
            K.op("act", lambda e: e.activation(out=sm[:, 2, :], in_=sm[:, 2, :], func=AF.Exp), reads=[smr], writes=[smr])
            K.op("act", lambda e: e.activation(out=sm[:, 3, :], in_=sm[:, 1, :], func=AF.Exp), reads=[smr], writes=[smr])
            K.op("act", lambda e: e.activation(out=sm[:, 5, :], in_=sm[:, 0, :], func=AF.Exp), reads=[smr], writes=[smr])
            K.op("dve", lambda e: e.tensor_tensor(out=sm[:, 4, :], in0=sm[:, 2, :], in1=dtT[:, c, :], op=ALU.mult), reads=[smr, dar], writes=[smr])
            for q in range(4):
                bk = K.banks[q % 2]
                for j in range(4):
                    K.op("pe", lambda e: e.transpose(out=bk.t[:, j * 128:(j + 1) * 128], in_=xa[:, 4 * q + j, :], identity=ident), reads=[xar, self.cres], writes=[bk.res])
                K.op("act", lambda e: e.copy(x_tm[:, q * 512:(q + 1) * 512], bk.t[:, :]), reads=[bk.res], writes=[xtr])
            x3 = x_tm[:].rearrange("p (h d) -> p h d", d=64)
            K.op("pool", lambda e: e.tensor_tensor(out=xw[:].rearrange("p (h d) -> p h d", d=64), in0=x3, in1=AP(sm, 4 * 32, [[192, 128], [1, 32], [0, 64]]), op=ALU.mult),
                 reads=[xtr, smr], writes=[xwr])
            if not states_only:
                K.op("pool", lambda e: e.tensor_tensor(out=xdt[:].rearrange("p (h d) -> p h d", d=64), in0=x3, in1=AP(dtT, c * 32, [[512, 128], [1, 32], [0, 64]]), op=ALU.mult),
                     reads=[xtr, dar], writes=[xdr])
                K.op("act", lambda e: e.copy(bcb[:], xa[:, 16:32, :]), reads=[xar], writes=[bcr])
            for q in range(2):
                bk = K.banks[2 + q]
                for j in range(4):
                    K.op("pe", lambda e: e.transpose(out=bk.t[:, j * 128:(j + 1) * 128], in_=xa[:, 16 + 4 * q + j, :], identity=ident), reads=[xar, self.cres], writes=[bk.res])
                K.op("act", lambda e: e.copy(Btm[:, 4 * q:4 * q + 4, :].rearrange("p a b -> p (a b)"), bk.t[:, :]), reads=[bk.res], writes=[btr])
            def front(g):
                pg = g % 2
                cbm, cbr = cbm2[:, pg, :], cbr2[pg]
                La, lar = La2[:, pg, :, :], lar2[pg]
                dec, der = dec2[:, pg, :, :], der2[pg]
                MT, mtr = MT2[:, pg, :, :], mtr2[pg]
                cbc = slice(128 + pg * 128, 256 + pg * 128)
                K.op("pe", lambda e: e.matmul(b6.t[:, cbc], lhsT=bcb[:, g, :], rhs=bcb[:, 8 + g, :], start=True, stop=True), reads=[bcr], writes=[cbres[pg]])
                K.op("dve", lambda e: e.tensor_tensor(out=cbm, in0=b6.t[:, cbc], in1=triU, op=ALU.mult), reads=[cbres[pg], self.cres], writes=[cbr])
                abc = AP(aT, c * 32 + 4 * g, [[512, 128], [1, 4], [0, 128]])
                K.op("pool", lambda e: e.tensor_tensor(out=La, in0=AP(self.c, C_MASKL, [[CST_W, 128], [0, 4], [1, 128]]), in1=abc, op=ALU.mult),
                     reads=[dar, self.cres], writes=[lar])
                b0 = K.banks[0 + pg]
                for j in range(4):
                    K.op("pe", lambda e: e.matmul(b0.t[:, j * 128:(j + 1) * 128], lhsT=La[:, j, :], rhs=triU, start=True, stop=True), reads=[lar, self.cres], writes=[b0.res])
                K.op("act", lambda e: e.activation(out=dec.rearrange("p a b -> p (a b)"), in_=b0.t[:, :], func=AF.Exp), reads=[b0.res], writes=[der])
                K.op("dve", lambda e: e.tensor_tensor(out=MT, in0=dec, in1=AP(cbm2, pg * 128, [[256, 128], [0, 4], [1, 128]]), op=ALU.mult), reads=[der, cbr], writes=[mtr])

            def back(g):
                gsl = slice(g * 256, (g + 1) * 256)
                pg = g % 2
                yb = K.banks[4 + (g // 2) % 2]
                yob = K.banks[2 + (g // 2) % 2]
                if not states_only:
                    MT, mtr = MT2[:, pg, :, :], mtr2[pg]
                    for j in range(4):
                        h = 4 * g + j
                        col = (g % 2) * 256 + j * 64
                        K.op("pe", lambda e: e.matmul(yb.t[:, col:col + 64], lhsT=MT[:, j, :], rhs=xdt[:, h * 64:(h + 1) * 64], start=True, stop=True),
                             reads=[mtr, xdr], writes=[yb.res])
                    K.op("pe", lambda e: e.matmul(yob.t[:, (g % 2) * 256:(g % 2) * 256 + 256], lhsT=bcb[:, 8 + g, :], rhs=prevb[:, gsl], start=True, stop=True),
                         reads=[bcr, pvrg[g]], writes=[yob.res])
                csb = K.banks[7]
                K.op("pe", lambda e: e.matmul(csb.t[:, 0:256], lhsT=Btm[:, g, :], rhs=xw[:, gsl], start=True, stop=True), reads=[btr, xwr], writes=[csb.res])
                K.op("dve", lambda e: e.tensor_tensor(out=stT[:, gsl].rearrange("p (h d) -> p h d", d=64), in0=stT[:, gsl].rearrange("p (h d) -> p h d", d=64),
                                                      in1=AP(sm, 3 * 32 + 4 * g, [[192, 128], [1, 4], [0, 64]]), op=ALU.mult), reads=[strg[g], smr], writes=[strg[g]])
                K.op("dve", lambda e: e.tensor_tensor(out=stT[:, gsl], in0=csb.t[:, 0:256], in1=stT[:, gsl], op=ALU.add), reads=[csb.res, strg[g]], writes=[strg[g]])
                if not states_only:
                    K.op("act", lambda e: e.copy(prevb[:, gsl], stT[:, gsl]), reads=[strg[g]], writes=[pvrg[g]])
                    if g % 2 == 1:
                        q = g // 2
                        bsl = slice(q * 512, (q + 1) * 512)
                        y3 = y[:, bsl].rearrange("p (h d) -> p h d", d=64)
                        K.op("dve", lambda e: e.tensor_tensor(out=y3, in0=x_tm[:, bsl].rearrange("p (h d) -> p h d", d=64),
                                                              in1=AP(Dh, 8 * q, [[32, 128], [1, 8], [0, 64]]), op=ALU.mult), reads=[xtr, ggr], writes=[yr])
                        K.op("dve", lambda e: e.tensor_tensor(out=y[:, bsl], in0=yb.t[:, :], in1=y[:, bsl], op=ALU.add), reads=[yb.res, yr], writes=[yr])
                        K.op("dve", lambda e: e.tensor_tensor(out=yo[:].rearrange("p (h d) -> p h d", d=64), in0=yob.t[:, :].rearrange("p (h d) -> p h d", d=64),
                                                              in1=AP(sm, 5 * 32 + 8 * q, [[192, 128], [1, 8], [0, 64]]), op=ALU.mult), reads=[yob.res, smr], writes=[yor])
                        K.op("dve", lambda e: e.tensor_tensor(out=y[:, bsl], in0=y[:, bsl], in1=yo[:], op=ALU.add), reads=[yr, yor], writes=[yr])
                        K.op("dve", lambda e: e.tensor_tensor(out=y[:, bsl], in0=y[:, bsl], in1=zc[:, bsl], op=ALU.mult), reads=[yr, zcr], writes=[yr])

            if not states_only:
                front(0)
            for g in range(8):
                if not states_only and g + 1 < 8:
                    front(g + 1)
                back(g)
            if states_only:
                continue
            for g8 in range(8):
                K.op("act", lambda e: e.activation(out=junk[:], in_=y[:, g8 * 256:(g8 + 1) * 256], func=AF.Square, accum_out=ss[:, g8:g8 + 1]), reads=[yr], writes=[ssr])
            K.op("act", lambda e: e.activation(out=ss[:, 8:16], in_=ss[:, 0:8], func=AF.Ln, scale=1.0 / 256, bias=EPS), reads=[ssr], writes=[ssr])
            K.op("act", lambda e: e.activation(out=ss[:, 8:16], in_=ss[:, 8:16], func=AF.Exp, scale=-0.5), reads=[ssr], writes=[ssr])
            for g8 in range(8):
                gs = slice(g8 * 256, (g8 + 1) * 256)
                K.op("dve", lambda e: e.scalar_tensor_tensor(out=y[:, gs], in0=y[:, gs], scalar=ss[:, 8 + g8:9 + g8], in1=gg[:, gs], op0=ALU.mult, op1=ALU.mult),
                     reads=[yr, ssr, ggr], writes=[yr])
            for q in range(4):
                bk = K.banks[2 + q % 2]
                for j in range(4):
                    fcx = 4 * q + j
                    K.op("pe", lambda e: e.transpose(out=bk.t[:, j * 128:(j + 1) * 128], in_=y[:, fcx * 128:(fcx + 1) * 128], identity=ident), reads=[yr, self.cres], writes=[bk.res])
                K.op("act", lambda e: e.copy(yTc[:, 4 * q:4 * q + 4, :].rearrange("p a b -> p (a b)"), bk.t[:, :]), reads=[bk.res], writes=[ytr])
            K.dma(AP(self.yT_d, c * 128, [[2048, 128], [128 * 2048, 16], [1, 128]]), yTc[:], reads=[ytr], writes=[self.dr("yT_d")])
        if states_only:
            K.dma(self.st_in.ap(), stT[:], reads=strg, writes=[self.dr("st_in")])
            K.coll(self.st_in, self.st_out, reads=[self.dr("st_in")], writes=[self.dr("st_out")])


C_ONES, C_IDENT, C_TRIU, C_MASKL = 0, 128, 256, 384
C_GAIN = 512
C_FLAG = C_GAIN + 72
C_SUBLN = C_FLAG + 1
C_DTB = C_SUBLN + 1
C_ALOG = C_DTB + 1
C_CONVW = C_ALOG + 1
C_CONVB = C_CONVW + 128
CST_W = C_CONVB + 32
R_LAM = 0
R_D = 256
R_GG = R_D + 2048
R_AREP = R_GG + 2048
R_DH = R_AREP + 32
REP_W = R_DH + 32
OH_W = 1536 + 4608


def rel_bucket_np(dist):
    dist = np.asarray(dist, dtype=np.int64)
    d = np.maximum(dist, 1).astype(np.float32)
    large = 16 + (np.log(d / np.float32(16)) / np.float32(math.log(2048 / 16)) * np.float32(16)).astype(np.int32)
    large = np.minimum(large, 31)
    return np.where(dist < 16, dist, large)


def build_onehot():
    oh = np.zeros((33, OH_W), np.float32)
    for g, D in enumerate(DILS):
        for which in range(2):
            for x in range(255):
                if which == 1:
                    steps = x - 127
                    valid = steps >= 0
                else:
                    steps = x + 1
                    valid = steps <= 128
                col = (g * 2 + which) * 256 + x
                if valid:
                    oh[int(rel_bucket_np(steps * D)), col] = 1.0
                else:
                    oh[32, col] = 1.0
            oh[32, (g * 2 + which) * 256 + 255] = 1.0
    xs = np.arange(4608)
    dist = xs - 511
    bk = rel_bucket_np(np.maximum(dist, 0))
    for x in range(4608):
        if dist[x] >= 0:
            oh[int(bk[x]), 1536 + x] = 1.0
        else:
            oh[32, 1536 + x] = 1.0
    return oh


def pack_F(W, KC):
    W = np.asarray(W, np.float32)
    n = W.shape[1] // 128
    return [W[:, o * 128:(o + 1) * 128].reshape(KC, 128, 128).transpose(1, 0, 2).reshape(128, KC * 128) for o in range(n)]


def units_F(W, KC):
    blocks = pack_F(W, KC)
    per = 2048 // (KC * 128)
    out = []
    for i in range(0, len(blocks), per):
        bl = blocks[i:i + per]
        while len(bl) < per:
            bl.append(np.zeros_like(bl[0]))
        out.append(np.concatenate(bl, axis=1))
    return out


def units_T(W):
    W = np.asarray(W, np.float32)
    n = W.shape[1] // 512
    out = []
    for c in range(n):
        blk = W[:, c * 512:(c + 1) * 512].reshape(8, 128, 512).transpose(1, 0, 2)
        out.append(np.ascontiguousarray(blk[:, 0:4, :]).reshape(128, 2048))
        out.append(np.ascontiguousarray(blk[:, 4:8, :]).reshape(128, 2048))
    return out


def build_stream(inp, layers, skip_mixer=False, skip_mlp=False):
    units = []
    for l in layers:
        pre = "l%d_" % l
        kind = l % 3
        if skip_mixer:
            pass
        elif kind == 0:
            W = inp[pre + "dil_w_qkv"].reshape(1024, 3, 3, 1024)
            for g in range(3):
                units += units_F(W[:, g, 1], 8)
                units += units_T(W[:, g, 2])
                units += units_F(W[:, g, 0], 8)
            units += units_F(inp[pre + "dil_w_o"], 8)
        elif kind == 1:
            W = inp[pre + "diff_w_qkv"]
            units += units_T(W[:, 2048:3072])
            units += units_F(W[:, 1024:2048], 8)
            units += units_F(W[:, 0:1024], 8)
            units += units_F(inp[pre + "diff_w_o"], 8)
        else:
            W = inp[pre + "ssm_w_in"]
            units += units_T(W[:, 0:2048])
            units += units_F(W[:, 2048:6144], 8)
            wdt = np.zeros((1024, 128), np.float32)
            wdt[:, 0:32] = W[:, 6144:6176]
            units += units_F(wdt, 8)
            units += units_F(inp[pre + "ssm_w_out"], 16)
        up, dn = inp[pre + "mlp_w_up"], inp[pre + "mlp_w_down"]
        for G in range(0 if skip_mlp else 4):
            units += units_F(up[:, G * 1024:(G + 1) * 1024], 8)
            units += units_F(dn[G * 1024:(G + 1) * 1024, :], 8)
    return np.ascontiguousarray(np.stack(units, axis=0)) if units else np.zeros((0, 128, 2048), np.float32)


def build_consts(inp, half):
    c = np.zeros((128, CST_W), np.float32)
    c[:, C_ONES:C_ONES + 128] = 1.0
    c[:, C_IDENT:C_IDENT + 128] = np.eye(128, dtype=np.float32)
    s = np.arange(128)[:, None]
    t = np.arange(128)[None, :]
    c[:, C_TRIU:C_TRIU + 128] = (s <= t)
    c[:, C_MASKL:C_MASKL + 128] = (s > t)
    names = ["l0_mix_norm", "l0_mlp_norm", "l1_mix_norm", "l1_mlp_norm", "l2_mix_norm", "l2_mlp_norm", "l3_mix_norm", "l3_mlp_norm", "final_norm"]
    for i, nm in enumerate(names):
        c[:, C_GAIN + i * 8:C_GAIN + i * 8 + 8] = np.asarray(inp[nm], np.float32).reshape(8, 128).T
    c[:, C_FLAG] = float(half)
    c[:, C_SUBLN] = np.asarray(inp["l1_diff_subln"], np.float32)
    c[0:32, C_DTB] = np.asarray(inp["l2_ssm_dt_bias"], np.float32)
    c[0:32, C_ALOG] = np.asarray(inp["l2_ssm_A_log"], np.float32)
    cw = np.asarray(inp["l2_ssm_conv_w"], np.float32)
    c[:, C_CONVW:C_CONVW + 128] = cw.reshape(4, 32, 128).transpose(2, 1, 0).reshape(128, 128)
    c[:, C_CONVB:C_CONVB + 32] = np.asarray(inp["l2_ssm_conv_b"], np.float32).reshape(32, 128).T
    r = np.zeros((128, REP_W), np.float32)
    lam = np.concatenate([np.asarray(inp["l1_diff_lam_" + k], np.float32) for k in ("q1", "k1", "q2", "k2")])
    r[:, R_LAM:R_LAM + 256] = lam[None, :]
    r[:, R_D:R_D + 2048] = np.repeat(np.asarray(inp["l2_ssm_D"], np.float32), 64)[None, :]
    r[:, R_GG:R_GG + 2048] = np.asarray(inp["l2_ssm_gate_norm"], np.float32)[None, :]
    r[:, R_AREP:R_AREP + 32] = np.asarray(inp["l2_ssm_A_log"], np.float32)[None, :]
    r[:, R_DH:R_DH + 32] = np.asarray(inp["l2_ssm_D"], np.float32)[None, :]
    return c, r


LAYERS = (0, 1, 2, 3)
_CACHE = {}


def run(inp, layers=LAYERS, skip_mixer=False, skip_mlp=False, stop=None):
    inp = {k: np.asarray(v) for k, v in inp.items()}
    wstream = build_stream(inp, layers, skip_mixer, skip_mlp)
    if wstream.shape[0] == 0:
        wstream = np.zeros((1, 128, 2048), np.float32)
    oh = build_onehot()
    key = (tuple(layers), wstream.shape[0], skip_mixer, skip_mlp, stop)
    if key not in _CACHE:
        _CACHE[key] = Builder(list(layers), wstream.shape[0], skip_mixer, skip_mlp, stop).build()
    nc = _CACHE[key]
    x = np.asarray(inp["x"], np.float32)
    relb = np.ascontiguousarray(np.asarray(inp["rel_bias"], np.float32))
    in_maps = []
    for core in range(8):
        b, half = core // 2, core % 2
        c, r = build_consts(inp, half)
        xT = np.ascontiguousarray(x[b, half * NT:(half + 1) * NT, :].T)
        in_maps.append({"xT": xT, "wstream": wstream, "cst": c, "rep": r, "oh": oh, "relb": relb})
    res = run_bass_kernel_spmd(nc, in_maps, core_ids=list(range(8)))
    out = np.zeros((4, 4096, 1024), np.float32)
    for core in range(8):
        b, half = core // 2, core % 2
        out[b, half * NT:(half + 1) * NT, :] = np.asarray(res.results[core]["outT"], np.float32).T
    return out


INPUT_NAMES = (
    "x", "rel_bias",
    "l0_mix_norm", "l0_dil_w_qkv", "l0_dil_w_o", "l0_mlp_norm", "l0_mlp_w_up", "l0_mlp_w_down",
    "l1_mix_norm", "l1_diff_w_qkv", "l1_diff_lam_q1", "l1_diff_lam_k1", "l1_diff_lam_q2", "l1_diff_lam_k2",
    "l1_diff_subln", "l1_diff_w_o", "l1_mlp_norm", "l1_mlp_w_up", "l1_mlp_w_down",
    "l2_mix_norm", "l2_ssm_w_in", "l2_ssm_conv_w", "l2_ssm_conv_b", "l2_ssm_dt_bias", "l2_ssm_A_log", "l2_ssm_D",
    "l2_ssm_gate_norm", "l2_ssm_w_out", "l2_mlp_norm", "l2_mlp_w_up", "l2_mlp_w_down",
    "l3_mix_norm", "l3_dil_w_qkv", "l3_dil_w_o", "l3_mlp_norm", "l3_mlp_w_up", "l3_mlp_w_down",
    "final_norm",
)


def kernel(**inputs):
    missing = [n for n in INPUT_NAMES if n not in inputs]
    assert not missing, missing
    return run(inputs, LAYERS)
```
